# Optimizing a Trainium2 kernel written in Bass

```python
import math
import jax, jax.numpy as jnp
from jax import lax
import numpy as np

D_MODEL = 2048
BATCH = 16
SEQ = 2048
DEPTH = 2

N_META = 16
Q_BLOCK = 128
HEAD_DIM = 128
FOX_HEADS = 6
FOX_W = FOX_HEADS * HEAD_DIM
DIFF_HEADS = 4
DIFF_QK = 64
DIFF_V = 128
DIFF_W = DIFF_HEADS * DIFF_V
CONV_CH = 768
CONV_WIDTH = 31
MIX_W = FOX_W + DIFF_W + CONV_CH
D_FF = 5632
FFN_CONV = 3
EPS = 1e-6
SUBLN_EPS = 1e-5
FOX_SCALE = HEAD_DIM ** -0.5
DIFF_SCALE = DIFF_QK ** -0.5

O_FQ = 0
O_FK = O_FQ + FOX_W
O_FV = O_FK + FOX_W
O_FF = O_FV + FOX_W
O_DQ = O_FF + FOX_HEADS
O_DK = O_DQ + DIFF_HEADS * 2 * DIFF_QK
O_DV = O_DK + DIFF_HEADS * 2 * DIFF_QK
O_CG = O_DV + DIFF_W
N_IN = O_CG + 2 * CONV_CH

kernel_name = "hymba_fox_conformer_diffattn_block"


def rmsnorm(x, g, eps=EPS):
    xf = x.astype(jnp.float32)
    y = xf * lax.rsqrt(jnp.mean(xf * xf, axis=-1, keepdims=True) + eps)
    return (y * g.astype(jnp.float32)).astype(x.dtype)


def layernorm(x, g, b):
    xf = x.astype(jnp.float32)
    mu = jnp.mean(xf, axis=-1, keepdims=True)
    xc = xf - mu
    y = xc * lax.rsqrt(jnp.mean(xc * xc, axis=-1, keepdims=True) + EPS)
    return (y * g.astype(jnp.float32) + b.astype(jnp.float32)).astype(x.dtype)


def causal_dwconv(x, w):
    k = w.shape[0]
    return lax.conv_general_dilated(
        x, w[:, None, :].astype(x.dtype), window_strides=(1,), padding=[(k - 1, 0)],
        dimension_numbers=("NWC", "WIO", "NWC"), feature_group_count=x.shape[-1])


def query_blocks(length):
    return [(0, N_META)] + [(s, min(s + Q_BLOCK, length)) for s in range(N_META, length, Q_BLOCK)]


def causal_mask(start, end):
    return jnp.arange(end)[None, :] <= jnp.arange(start, end)[:, None]


def fox_attention(q, k, v, log_f):
    c = jnp.transpose(jnp.cumsum(log_f, axis=1), (0, 2, 1))
    outs = []
    for start, end in query_blocks(q.shape[1]):
        s = jnp.einsum("bqhd,bkhd->bhqk", q[:, start:end], k[:, :end],
                       preferred_element_type=jnp.float32) * FOX_SCALE
        s = s + c[:, :, start:end, None] - c[:, :, None, :end]
        s = jnp.where(causal_mask(start, end), s, -jnp.inf)
        p = jax.nn.softmax(s, axis=-1)
        outs.append(jnp.einsum("bhqk,bkhd->bqhd", p.astype(v.dtype), v[:, :end]))
    return jnp.concatenate(outs, axis=1)


def diff_attention(q1, q2, k1, k2, v, lam):
    outs = []
    for start, end in query_blocks(q1.shape[1]):
        mask = causal_mask(start, end)
        s1 = jnp.einsum("bqhd,bkhd->bhqk", q1[:, start:end], k1[:, :end],
                        preferred_element_type=jnp.float32) * DIFF_SCALE
        s2 = jnp.einsum("bqhd,bkhd->bhqk", q2[:, start:end], k2[:, :end],
                        preferred_element_type=jnp.float32) * DIFF_SCALE
        p1 = jax.nn.softmax(jnp.where(mask, s1, -jnp.inf), axis=-1)
        p2 = jax.nn.softmax(jnp.where(mask, s2, -jnp.inf), axis=-1)
        p = p1 - lam * p2
        outs.append(jnp.einsum("bhqk,bkhd->bqhd", p.astype(v.dtype), v[:, :end]))
    return jnp.concatenate(outs, axis=1)


def hybrid_mixer(h, layer, w_in, b_f, lam_q1, lam_k1, lam_q2, lam_k2, g_sub,
                 w_dw, b_dw, ln_g, ln_b, w_out):
    bsz, length, _ = h.shape
    proj = h @ w_in

    fq = proj[..., O_FQ:O_FK].reshape(bsz, length, FOX_HEADS, HEAD_DIM)
    fk = proj[..., O_FK:O_FV].reshape(bsz, length, FOX_HEADS, HEAD_DIM)
    fv = proj[..., O_FV:O_FF].reshape(bsz, length, FOX_HEADS, HEAD_DIM)
    log_f = jax.nn.log_sigmoid((proj[..., O_FF:O_DQ] + b_f).astype(jnp.float32))
    y_fox = fox_attention(fq, fk, fv, log_f).reshape(bsz, length, FOX_W)

    dq = proj[..., O_DQ:O_DK].reshape(bsz, length, DIFF_HEADS, 2, DIFF_QK)
    dk = proj[..., O_DK:O_DV].reshape(bsz, length, DIFF_HEADS, 2, DIFF_QK)
    dv = proj[..., O_DV:O_CG].reshape(bsz, length, DIFF_HEADS, DIFF_V)
    lam_init = 0.8 - 0.6 * math.exp(-0.3 * layer)
    lam = (jnp.exp(jnp.sum(lam_q1.astype(jnp.float32) * lam_k1.astype(jnp.float32)))
           - jnp.exp(jnp.sum(lam_q2.astype(jnp.float32) * lam_k2.astype(jnp.float32)))
           + lam_init)
    y_diff = diff_attention(dq[..., 0, :], dq[..., 1, :], dk[..., 0, :], dk[..., 1, :], dv, lam)
    y_diff = (rmsnorm(y_diff, g_sub, SUBLN_EPS) * (1.0 - lam_init)).reshape(bsz, length, DIFF_W)

    a = proj[..., O_CG:O_CG + CONV_CH]
    gt = proj[..., O_CG + CONV_CH:N_IN]
    u = a * jax.nn.sigmoid(gt)
    u = causal_dwconv(u, w_dw) + b_dw
    y_conv = jax.nn.silu(layernorm(u, ln_g, ln_b))

    y = jnp.concatenate([y_fox, y_diff, y_conv], axis=-1)
    return y @ w_out


def conv_gated_ffn(h, w_gate, w_up, w_conv, w_down):
    g = causal_dwconv(h @ w_gate, w_conv)
    return (jax.nn.silu(g) * (h @ w_up)) @ w_down


def setup_inputs(seed: int = 0) -> dict:
    key = jax.random.key(seed)
    ks = jax.random.split(key, 24)
    f32 = jnp.float32

    def nrm(k, shape, scale):
        return jax.random.normal(k, shape, f32) * scale

    def gain(k, shape):
        return 1.0 + 0.05 * jax.random.normal(k, shape, f32)

    return {
        "x": nrm(ks[0], (BATCH, SEQ, D_MODEL), 1.0),
        "meta_tokens": nrm(ks[1], (N_META, D_MODEL), 1.0),
        "w_in": nrm(ks[2], (DEPTH, D_MODEL, N_IN), D_MODEL ** -0.5),
        "b_f": nrm(ks[3], (DEPTH, FOX_HEADS), 0.1),
        "lam_q1": nrm(ks[4], (DEPTH, DIFF_QK), 0.1),
        "lam_k1": nrm(ks[5], (DEPTH, DIFF_QK), 0.1),
        "lam_q2": nrm(ks[6], (DEPTH, DIFF_QK), 0.1),
        "lam_k2": nrm(ks[7], (DEPTH, DIFF_QK), 0.1),
        "g_sub": gain(ks[8], (DEPTH, DIFF_V)),
        "w_dw": nrm(ks[9], (DEPTH, CONV_WIDTH, CONV_CH), CONV_WIDTH ** -0.5),
        "b_dw": nrm(ks[10], (DEPTH, CONV_CH), 0.02),
        "ln_g": gain(ks[11], (DEPTH, CONV_CH)),
        "ln_b": nrm(ks[12], (DEPTH, CONV_CH), 0.02),
        "w_out": nrm(ks[13], (DEPTH, MIX_W, D_MODEL), MIX_W ** -0.5),
        "w_gate": nrm(ks[14], (DEPTH, D_MODEL, D_FF), D_MODEL ** -0.5),
        "w_up": nrm(ks[15], (DEPTH, D_MODEL, D_FF), D_MODEL ** -0.5),
        "w_ffn_conv": nrm(ks[16], (DEPTH, FFN_CONV, D_FF), FFN_CONV ** -0.5),
        "w_down": nrm(ks[17], (DEPTH, D_FF, D_MODEL), D_FF ** -0.5),
        "g_pre_mix": gain(ks[18], (DEPTH, D_MODEL)),
        "g_post_mix": gain(ks[19], (DEPTH, D_MODEL)),
        "g_pre_ffn": gain(ks[20], (DEPTH, D_MODEL)),
        "g_post_ffn": gain(ks[21], (DEPTH, D_MODEL)),
    }


def reference(x, meta_tokens, w_in, b_f, lam_q1, lam_k1, lam_q2, lam_k2, g_sub,
              w_dw, b_dw, ln_g, ln_b, w_out, w_gate, w_up, w_ffn_conv, w_down,
              g_pre_mix, g_post_mix, g_pre_ffn, g_post_ffn):
    bsz = x.shape[0]
    meta = jnp.broadcast_to(meta_tokens.astype(x.dtype)[None], (bsz, N_META, x.shape[-1]))
    h = jnp.concatenate([meta, x], axis=1)
    for layer in range(DEPTH):
        m = hybrid_mixer(rmsnorm(h, g_pre_mix[layer]), layer, w_in[layer], b_f[layer],
                         lam_q1[layer], lam_k1[layer], lam_q2[layer], lam_k2[layer],
                         g_sub[layer], w_dw[layer], b_dw[layer], ln_g[layer], ln_b[layer],
                         w_out[layer])
        h = h + rmsnorm(m, g_post_mix[layer])
        f = conv_gated_ffn(rmsnorm(h, g_pre_ffn[layer]), w_gate[layer], w_up[layer],
                           w_ffn_conv[layer], w_down[layer])
        h = h + rmsnorm(f, g_post_ffn[layer])
    return h[:, N_META:]
```

```python
import contextlib
import math
import numpy as np
import concourse.bass as bass
import concourse.mybir as mybir
from concourse.bass_utils import run_bass_kernel_spmd

F32 = mybir.dt.float32
BF16 = mybir.dt.bfloat16
AF = mybir.ActivationFunctionType
ALU = mybir.AluOpType
AX = mybir.AxisListType

D = 2048
KC = 16
NIN = 5382
DFF = 5632
FCH = 44
O_FQ, O_FK, O_FV, O_FF, O_DQ, O_DK, O_DV, O_CG = 0, 768, 1536, 2304, 2310, 2822, 3334, 3846
CONVW = 31
EPS = 1e-6
SUBLN_EPS = 1e-5
FOX_SCALE = 128 ** -0.5
DIFF_SCALE = 64 ** -0.5
DEPTH = 2
NEG = -30000.0


class Tok:
    __slots__ = ("sem", "val")

    def __init__(self, sem, val):
        self.sem = sem
        self.val = val


class Buf:
    __slots__ = ("name", "w", "r")

    def __init__(self, name=""):
        self.name = name
        self.w = None
        self.r = []


class Chan:
    __slots__ = ("sem", "count", "last")

    def __init__(self, sem):
        self.sem = sem
        self.count = 0
        self.last = None


class Eng:
    def __init__(self, name, sem):
        self.name = name
        self.sem = sem
        self.cnt = 0
        self.seen = {}
        self.ops = []


class Prog:
    def __init__(self, nc, stack):
        self.nc = nc
        self.stack = stack
        self.nsem = 0
        self.E = {}
        for name in ("pe", "act", "dve", "pool", "sp"):
            self.E[name] = Eng(name, self.new_sem("e_" + name))
        self.chans = []
        self.nops = 0

    def new_sem(self, name):
        self.nsem += 1
        return self.stack.enter_context(self.nc.semaphore(name))

    def chan(self):
        c = Chan(self.new_sem("ch%d" % self.nsem))
        self.chans.append(c)
        return c

    def _deps(self, e, reads, writes, extra=()):
        deps = {}

        def need(t):
            if t is not None:
                k = id(t.sem)
                if k not in deps or deps[k][1] < t.val:
                    deps[k] = (t.sem, t.val)

        for b in reads:
            need(b.w)
        for b in writes:
            need(b.w)
            for t in b.r:
                need(t)
        for t in extra:
            need(t)
        waits = []
        for k, (sem, val) in deps.items():
            if e.seen.get(k, 0) < val:
                e.seen[k] = val
                waits.append((sem, val))
        return waits

    def op(self, eng, fn, reads=(), writes=(), inc=True):
        e = self.E[eng]
        waits = self._deps(e, reads, writes)
        if eng == "pe":
            waits = [(s, v) for (s, v) in waits if s is not e.sem]
        if inc:
            e.cnt += 1
        tok = Tok(e.sem, e.cnt if inc else e.cnt + 1)
        e.ops.append((waits, fn, (e.sem, 1) if inc else None))
        for b in reads:
            b.r.append(tok)
            if len(b.r) > 64:
                b.r = b.r[-48:] if False else b.r
        for b in writes:
            b.w = tok
            b.r = []
        self.nops += 1
        return tok

    def dma(self, eng, fn, ch, reads=(), writes=()):
        e = self.E[eng]
        waits = self._deps(e, reads, writes, extra=(ch.last,))
        ch.count += 16
        tok = Tok(ch.sem, ch.count)
        ch.last = tok
        e.ops.append((waits, fn, (ch.sem, 16)))
        for b in reads:
            b.r.append(tok)
        for b in writes:
            b.w = tok
            b.r = []
        self.nops += 1
        return tok

    def barrier(self):
        allw = []
        for en in self.E.values():
            if en.cnt > 0:
                allw.append((en.sem, en.cnt))
        for c in self.chans:
            if c.last is not None:
                allw.append((c.sem, c.count))
        for en in self.E.values():
            waits = []
            for (s, v) in allw:
                if en.seen.get(id(s), 0) < v:
                    en.seen[id(s)] = v
                    waits.append((s, v))
            if waits:
                en.ops.append((waits, None, None))

    def emit(self):
        nc = self.nc
        self.barrier()
        with nc.Block() as block:
            def mk(en):
                def body(h):
                    for waits, fn, inc in en.ops:
                        for (s, v) in waits:
                            h.wait_ge(s, v)
                        if fn is not None:
                            ins = fn(h)
                            if inc is not None:
                                ins.then_inc(inc[0], inc[1])
                return body
            block.tensor(mk(self.E["pe"]))
            block.scalar(mk(self.E["act"]))
            block.vector(mk(self.E["dve"]))
            block.gpsimd(mk(self.E["pool"]))
            block.sync(mk(self.E["sp"]))


class Ring:
    def __init__(self, items):
        self.items = items
        self.i = 0

    def next(self):
        it = self.items[self.i % len(self.items)]
        self.i += 1
        return it


class Builder:
    def __init__(self, NS, SEQ, depth=DEPTH, stop_after=None, debug=False):
        self.NS, self.SEQ, self.depth = NS, SEQ, depth
        self.debug = debug
        self.fuse_wout = (stop_after is None and NS == 2)
        self.L = 16 + SEQ
        self.stop_after = stop_after
        self.chunks = [(0, 16)] + [(16 + 128 * i, 128) for i in range(SEQ // 128)]
        self.wtiles = [(0, 16)] + [(16 + 512 * j, 512) for j in range(SEQ // 512)]
        self.NCH = len(self.chunks)
        self.nc = bass.Bass("TRN2", target_bir_lowering=False)

    def din(self, name, shape, dt=F32):
        return self.nc.dram_tensor(name, list(shape), dt, kind="ExternalInput").ap()

    def dscr(self, name, shape, dt):
        return self.nc.dram_tensor(name, list(shape), dt, kind="ExternalOutput" if self.debug else "Internal").ap()

    def sb(self, st, name, shape, dt):
        self._n = getattr(self, "_n", 0) + 1
        t = st.enter_context(self.nc.sbuf_tensor("%s_%d" % (name, self._n), list(shape), dt))
        return t, Buf(name)

    def ring(self, st, name, shape, dt, n):
        return Ring([self.sb(st, name + str(i), shape, dt) for i in range(n)])

    def mm(self, out, lhsT, rhs, start, stop, reads, writes, inc):
        self.P.op("pe", lambda h: h.matmul(out, lhsT=lhsT, rhs=rhs, start=start, stop=stop),
                  reads, writes, inc=inc)

    def tr(self, out, in_, ident, reads, writes, inc=True):
        self.P.op("pe", lambda h: h.transpose(out, in_, ident), reads, writes, inc=inc)

    def act(self, out, in_, func, reads, writes, bias=None, scale=None, accum=None):
        kw = {}
        if bias is not None:
            kw["bias"] = bias
        if scale is not None:
            kw["scale"] = scale
        if accum is not None:
            kw["accum_out"] = accum
        self.P.op("act", lambda h: h.activation(out, in_, func, **kw), reads, writes)

    def tt(self, eng, out, in0, in1, op, reads, writes):
        self.P.op(eng, lambda h: h.tensor_tensor(out, in0, in1, op), reads, writes)

    def ts(self, eng, out, in0, s1, s2, op0, op1, reads, writes):
        if s2 is None:
            self.P.op(eng, lambda h: h.tensor_scalar(out, in0, s1, None, op0), reads, writes)
        else:
            self.P.op(eng, lambda h: h.tensor_scalar(out, in0, s1, s2, op0, op1), reads, writes)

    def stt(self, out, in0, scalar, in1, op0, op1, reads, writes, accum=None):
        if accum is None:
            self.P.op("dve", lambda h: h.scalar_tensor_tensor(out, in0, scalar, in1, op0, op1), reads, writes)
        else:
            self.P.op("dve", lambda h: h.scalar_tensor_tensor(out, in0, scalar, in1, op0, op1, accum),
                      reads, writes)

    def cp(self, eng, out, in_, reads, writes):
        if eng == "act":
            self.P.op("act", lambda h: h.activation(out, in_, AF.Copy), reads, writes)
        else:
            self.P.op(eng, lambda h: h.tensor_copy(out, in_), reads, writes)

    def memset(self, eng, ap, val, writes):
        self.P.op(eng, lambda h: h.memset(ap, val), (), writes)

    def dma(self, q, out, in_, ch, reads, writes, **kw):
        return self.P.dma(q, lambda h: h.dma_start(out=out, in_=in_, **kw), ch, reads, writes)

    def wload(self, out, in_, ch, wbuf):
        return self.dma("pool", out, in_, ch, (), [wbuf])

    def rstd(self, out, ss, scale, eps, reads_b, out_b, tmp, tmp_b):
        n = ss.shape[0]
        self.act(tmp, ss, AF.Ln, [reads_b], [tmp_b], bias=self.epsc[eps][0:n, :], scale=scale)
        self.act(out, tmp, AF.Exp, [tmp_b], [out_b], scale=-0.5)

    def build(self):
        nc = self.nc
        NS, SEQ, L = self.NS, self.SEQ, self.L
        self.x = self.din("x", [NS, SEQ, D])
        self.meta = self.din("meta_tokens", [16, D])
        self.w_in = self.din("w_in", [DEPTH, D, NIN])
        self.b_f = self.din("b_f", [DEPTH, 6])
        self.lam = [self.din(n, [DEPTH, 64]) for n in ("lam_q1", "lam_k1", "lam_q2", "lam_k2")]
        self.g_sub = self.din("g_sub", [DEPTH, 128])
        self.w_dw = self.din("w_dw", [DEPTH, CONVW, 768])
        self.b_dw = self.din("b_dw", [DEPTH, 768])
        self.ln_g = self.din("ln_g", [DEPTH, 768])
        self.ln_b = self.din("ln_b", [DEPTH, 768])
        self.w_out = self.din("w_out", [DEPTH, D, D])
        self.w_gate = self.din("w_gate", [DEPTH, D, DFF])
        self.w_up = self.din("w_up", [DEPTH, D, DFF])
        self.w_fc = self.din("w_ffn_conv", [DEPTH, 3, DFF])
        self.w_down = self.din("w_down", [DEPTH, DFF, D])
        self.gn = {n: self.din(n, [DEPTH, D]) for n in ("g_pre_mix", "g_post_mix", "g_pre_ffn", "g_post_ffn")}
        self.c_ident = self.din("c_ident", [128, 128])
        self.c_mask = self.din("c_mask", [128, 128])
        self.c_sel3 = self.din("c_sel3", [70, 768])
        self.out = nc.dram_tensor("out", [NS, SEQ, D], F32, kind="ExternalOutput").ap()
        self.hres = self.dscr("hres", [NS, L, D], F32)
        self.fraw = self.dscr("fraw", [NS, L, D], F32)
        self.hnT_scr = self.dscr("hnT_scr", [NS, 128, KC, L], BF16)
        self.yT_scr = self.dscr("yT_scr", [NS, 128, KC, L], BF16)
        self.aT_scr = self.dscr("aT_scr", [NS, self.NCH, 128, FCH, 128], BF16)
        self.U_scr = self.dscr("U_scr", [NS, 128, 6, 30 + L], BF16)
        self.b_U = [Buf("Us") for _ in range(NS)]
        self.b_hres = [Buf("hres") for _ in range(NS)]
        self.b_fraw = [Buf("fraw") for _ in range(NS)]
        self.b_hnT = [Buf("hnTs") for _ in range(NS)]
        self.b_yT = [Buf("yTs") for _ in range(NS)]
        self.b_aT = [Buf("aTs") for _ in range(NS)]
        self.b_out = Buf("out")

        with contextlib.ExitStack() as st:
            self.P = P = Prog(nc, st)
            self.ps = []
            for i in range(6):
                t = st.enter_context(nc.psum_tensor("psb%d" % i, [128, 512], F32))
                self.ps.append((t, Buf("psb%d" % i)))
            self.ptr = []
            for i in range(2):
                t = st.enter_context(nc.psum_tensor("ptr%d" % i, [128, 8, 128], BF16))
                self.ptr.append((t, Buf("ptr%d" % i)))
            self.psring = Ring(self.ps)
            self.ptring = Ring(self.ptr)
            self.ptfring = Ring(self.ps[4:6])
            ch0 = P.chan()
            self.chs = [P.chan() for _ in range(8)]
            self.chw = [P.chan() for _ in range(6)]
            self._chi = 0
            self._chwi = 0
            tmpf, tmpf_b = self.sb(st, "tmpf", [128, 768], F32)
            self.ident, self.ident_b = self.sb(st, "ident", [128, 128], BF16)
            self.identf, self.identf_b = self.sb(st, "identf", [128, 128], F32)
            self.maskT, self.maskT_b = self.sb(st, "maskT", [128, 128], BF16)
            self.sel3, self.sel3_b = self.sb(st, "sel3", [70, 768], BF16)
            self.onesf, self.onesf_b = self.sb(st, "onesf", [128, 128], F32)
            self.dma("sp", self.identf[:], self.c_ident[:, :], ch0, (), [self.identf_b])
            self.cp("dve", self.ident[:], self.identf[:], [self.identf_b], [self.ident_b])
            self.dma("sp", tmpf[:, 0:128], self.c_mask[:, :], ch0, (), [tmpf_b])
            self.cp("dve", self.maskT[:], tmpf[:, 0:128], [tmpf_b], [self.maskT_b])
            self.dma("sp", tmpf[0:70, :], self.c_sel3[:, :], ch0, [], [tmpf_b])
            self.cp("dve", self.sel3[:], tmpf[0:70, :], [tmpf_b], [self.sel3_b])
            self.memset("dve", self.onesf[:], 1.0, [self.onesf_b])
            self.epsc = {}
            self.eps_b = Buf("eps")
            for e in (EPS, SUBLN_EPS, 1.0):
                t, _ = self.sb(st, "epsc", [128, 1], F32)
                self.memset("dve", t[:], e, [self.eps_b])
                self.epsc[e] = t[:]
            self.neglam = []
            self.lam_b = Buf("lam")
            lt, lt_b = self.sb(st, "lamt", [128, 4, 64], F32)
            lj, lj_b = self.sb(st, "lamj", [128, 64], F32)
            for l in range(self.depth):
                for i in range(4):
                    self.dma("sp", lt[:, i, :], self.lam[i][l:l + 1, :].partition_broadcast(128), ch0, (), [lt_b])
                s2, s2_b = self.sb(st, "lams", [128, 2], F32)
                self.stt(lj[:], lt[:, 0, :], 1.0, lt[:, 1, :], ALU.mult, ALU.mult, [lt_b], [lj_b, s2_b], accum=s2[:, 0:1])
                self.stt(lj[:], lt[:, 2, :], 1.0, lt[:, 3, :], ALU.mult, ALU.mult, [lt_b], [lj_b, s2_b], accum=s2[:, 1:2])
                e2, e2_b = self.sb(st, "lame", [128, 2], F32)
                self.act(e2[:], s2[:], AF.Exp, [s2_b], [e2_b])
                nl, _ = self.sb(st, "neglam", [128, 1], F32)
                lam_init = 0.8 - 0.6 * math.exp(-0.3 * l)
                self.tt("dve", nl[:], e2[:, 1:2], e2[:, 0:1], ALU.subtract, [e2_b], [self.lam_b])
                self.ts("dve", nl[:], nl[:], -lam_init, None, ALU.add, None, [self.lam_b], [self.lam_b])
                self.neglam.append(nl)
            P.barrier()

            self.schedule()
            P.emit()
        return nc

    def schedule(self):
        NS, depth = self.NS, self.depth
        stop = self.stop_after
        if stop is not None or NS != 2:
            for s in range(NS):
                self.phase_norm(s, 0, first=True)
                for l in range(depth):
                    for name, fn in (("mixer", self.phase_mixer), ("wout", self.phase_wout),
                                     ("ffn1", self.phase_ffn1), ("down", self.phase_down)):
                        fn(s, l)
                        if stop == (s, l, name):
                            return
                    self.phase_norm(s, l + 1, first=False)
                    if stop == (s, l, "norm"):
                        return
            return
        self.phase_norm(0, 0, first=True)
        for l in range(depth):
            for s in range(2):
                self.phase_mixer(s, l)
                self.phase_wout(s, l)
                self.phase_ffn1(s, l)
                if s == 0:
                    nrm = (1, 0, True) if l == 0 else (1, l, False)
                else:
                    nrm = (0, l + 1, False)
                self.phase_down(s, l, norm=nrm)
        self.phase_norm(1, depth, first=False)

    def chn(self):
        self._chi += 1
        return self.chs[self._chi % len(self.chs)]

    def chwn(self):
        self._chwi += 1
        return self.chw[self._chwi % len(self.chw)]

    def gload(self, st, name, l):
        t, b = self.sb(st, name, [128, D], F32)
        self.dma("sp", t[:], self.gn[name][l:l + 1, :].partition_broadcast(128), self.chn(), (), [b])
        return t, b

    def norm_steps(self, st, s, l, first, nbuf=3):
        last = (l == self.depth)
        hring = self.ring(st, "nh", [128, D], F32, nbuf)
        fring = self.ring(st, "nf", [128, D], F32, nbuf) if not first else None
        yring = self.ring(st, "ny", [128, D], BF16, 2)
        stg = self.ring(st, "nstg", [128, KC, 128], BF16, 2)
        junk, junk_b = self.sb(st, "njunk", [128, D], BF16)
        smr = self.ring(st, "nsm", [128, 8], F32, 3)
        gpost = gpost_b = gpre = gpre_b = None
        if not first:
            gpost, gpost_b = self.gload(st, "g_post_ffn", l - 1)
        if not last:
            gpre, gpre_b = self.gload(st, "g_pre_mix", l)

        def pre_n(ci):
            t0, n = self.chunks[ci]
            h, h_b = hring.next()
            f = f_b = None
            if first:
                src = self.meta[:, :] if ci == 0 else self.x[s, t0 - 16:t0 - 16 + n, :]
                self.dma("sp", h[0:n, :], src, self.chn(), (), [h_b])
            else:
                f, f_b = fring.next()
                self.dma("sp", h[0:n, :], self.hres[s, t0:t0 + n, :], self.chn(), [self.b_hres[s]], [h_b])
                self.dma("sp", f[0:n, :], self.fraw[s, t0:t0 + n, :], self.chn(), [self.b_fraw[s]], [f_b])
            return h, h_b, f, f_b
        state = {"nxt": None, "pend": None}

        def step(ci):
            t0, n = self.chunks[ci]
            if ci == 0:
                state["nxt"] = pre_n(0)
            h, h_b, f, f_b = state["nxt"]
            if ci + 1 < self.NCH:
                state["nxt"] = pre_n(ci + 1)
            if state["pend"] is not None:
                state["pend"]()
                state["pend"] = None
            sm, sm_b = smr.next()
            if not first:
                self.act(junk[0:n, :], f[0:n, :], AF.Square, [f_b], [junk_b, sm_b], accum=sm[0:n, 0:1])
                self.rstd(sm[0:n, 2:3], sm[0:n, 0:1], 1.0 / D, EPS, sm_b, sm_b, sm[0:n, 1:2], sm_b)
                self.stt(f[0:n, :], f[0:n, :], sm[0:n, 2:3], gpost[0:n, :], ALU.mult, ALU.mult,
                         [sm_b, gpost_b, f_b], [f_b])
                self.tt("dve", h[0:n, :], h[0:n, :], f[0:n, :], ALU.add, [f_b, h_b], [h_b])
            if last:
                if ci > 0:
                    self.dma("sp", self.out[s, t0 - 16:t0 - 16 + n, :], h[0:n, :], self.chn(), [h_b], [self.b_out])
                return
            self.dma("sp", self.hres[s, t0:t0 + n, :], h[0:n, :], self.chn(), [h_b], [self.b_hres[s]])
            state["pend"] = self.prenorm_T(h, h_b, n, t0, gpre, gpre_b, junk, junk_b, sm, sm_b, yring, stg,
                                           self.hnT_scr[s], self.b_hnT[s], defer=True)

        def fin():
            if state["pend"] is not None:
                state["pend"]()
                state["pend"] = None
        return [(lambda ci=ci: step(ci)) for ci in range(self.NCH)] + [fin]

    def phase_norm(self, s, l, first):
        with contextlib.ExitStack() as st:
            for st_ in self.norm_steps(st, s, l, first):
                st_()
        self.P.barrier()

    def prenorm_T(self, h, h_b, n, t0, g, g_b, junk, junk_b, sm, sm_b, yring, stg, dstT, dst_b, defer=False):
        y, y_b = yring.next()
        self.act(junk[0:n, :], h[0:n, :], AF.Square, [h_b], [junk_b, sm_b], accum=sm[0:n, 4:5])
        self.rstd(sm[0:n, 6:7], sm[0:n, 4:5], 1.0 / D, EPS, sm_b, sm_b, sm[0:n, 5:6], sm_b)
        self.stt(y[0:n, :], h[0:n, :], sm[0:n, 6:7], g[0:n, :], ALU.mult, ALU.mult, [sm_b, g_b, h_b], [y_b])
        def fin():
            sg, sg_b = stg.next()
            for half in range(2):
                pt, pt_b = self.ptring.next()
                for k in range(8):
                    kc = half * 8 + k
                    self.tr(pt[:, k, 0:n], y[0:n, kc * 128:(kc + 1) * 128], self.ident[0:n, 0:n],
                            [y_b, self.ident_b], [pt_b], inc=(k == 7))
                self.cp("act" if half == 0 else "dve", sg[:, half * 8:(half + 1) * 8, 0:n], pt[:, :, 0:n], [pt_b], [sg_b])
            self.dma("sp", dstT[:, :, t0:t0 + n], sg[:, :, 0:n], self.chn(), [sg_b], [dst_b])
        if defer:
            return fin
        fin()
        return None

    def attn_steps(self, qT, q_b, kT, k_b, p0, K, vaug, v_b, ptring, bias_h, cscol, cs_b, C3, C3_b, consumer,
                   sring, accring):
        steps = []
        pending = []

        def advance():
            for ent in list(pending):
                ent.pop(0)()
                if not ent:
                    pending.remove(ent)

        def flush():
            while pending:
                advance()

        for j, (q0, nq) in enumerate(self.wtiles):
            if j == 0:
                klist = [(0, 0, True)]
            else:
                klist = [(c, 0, False) for c in range(0, 4 * (j - 1) + 1)]
                klist += [(4 * (j - 1) + 1 + m, 128 * m, True) for m in range(4)]
            cell = {}

            def s1(c, col0, diag, first, j=j, q0=q0, nq=nq, cell=cell):
                if first:
                    cell["PT"] = ptring.next()
                PT, PT_b = cell["PT"]
                k0, nk = self.chunks[c]
                sps, sps_b = sring.next()
                last_plain = (bias_h is None) and (not diag)
                self.mm(sps[0:nk, col0:nq], kT[p0:p0 + K, k0:k0 + nk], qT[p0:p0 + K, q0 + col0:q0 + nq],
                        True, last_plain, [k_b, q_b], [sps_b], inc=last_plain)
                if bias_h is not None:
                    self.mm(sps[0:nk, col0:nq], self.sel3[0:70, bias_h * 128:bias_h * 128 + nk],
                            C3[0:70, q0 + col0:q0 + nq], False, not diag, [self.sel3_b, C3_b], [sps_b], inc=not diag)
                if diag:
                    self.mm(sps[0:nk, col0:col0 + nk], self.ident[0:nk, 0:nk], self.maskT[0:nk, 0:nk],
                            False, True, [self.ident_b, self.maskT_b], [sps_b], inc=True)
                if bias_h is not None:
                    self.act(PT[0:nk, c, col0:nq], sps[0:nk, col0:nq], AF.Exp, [sps_b, cs_b], [PT_b],
                             bias=cscol[0:nk, c, bias_h:bias_h + 1])
                else:
                    self.act(PT[0:nk, c, col0:nq], sps[0:nk, col0:nq], AF.Exp, [sps_b], [PT_b])
                advance()

            for i, (c, col0, diag) in enumerate(klist):
                steps.append(lambda c=c, col0=col0, diag=diag, first=(i == 0), s1=s1: s1(c, col0, diag, first))

            subs = [(0, 0)] if j == 0 else [(4 * (j - 1) + 1 + b, b) for b in range(4)]

            def s2(ci, b, cell=cell):
                PT, PT_b = cell["PT"]
                t0, n = self.chunks[ci]
                acc, acc_b = accring.next()
                for c in range(ci + 1):
                    k0, nk = self.chunks[c]
                    self.mm(acc[0:n, 0:129], PT[0:nk, c, b * 128:b * 128 + n], vaug[0:nk, c, 0:129],
                            c == 0, c == ci, [PT_b, v_b], [acc_b], inc=(c == ci))
                d = consumer(ci, t0, n, acc, acc_b)
                advance()
                if d:
                    pending.append(list(d))

            for (ci, b) in subs:
                steps.append(lambda ci=ci, b=b, s2=s2: s2(ci, b))
        steps.append(flush)
        return steps

    @staticmethod
    def run_merged(a_steps, b_steps):
        na, nb = len(a_steps), len(b_steps)
        bi = 0
        for i, st_ in enumerate(a_steps):
            st_()
            while bi < nb and (bi + 1) * na <= (i + 1) * nb:
                b_steps[bi]()
                bi += 1
        while bi < nb:
            b_steps[bi]()
            bi += 1

    def phase_mixer(self, s, l):
        P = self.P
        L, NCH = self.L, self.NCH
        w_in = self.w_in[l].rearrange("(kc p) n -> p kc n", p=128)
        with contextlib.ExitStack() as st:
            ystage = self.ring(st, "ystage", [128, L], BF16, 2)
            smr = self.ring(st, "msm", [128, 8], F32, 8)
            cscol, cs_b = self.sb(st, "cscol", [128, NCH, 6], F32)
            C3, C3_b = self.sb(st, "C3", [70, L], BF16)
            sth = contextlib.ExitStack()
            hnT, hn_b = self.sb(sth, "hnT", [128, KC, L], BF16)
            self.dma("sp", hnT[:], self.hnT_scr[s], self.chn(), [self.b_hnT[s]], [hn_b])

            def proj_ws(wslab, w_b, M, evac):
                for (t0, n) in self.wtiles:
                    ps, ps_b = self.psring.next()
                    for kc in range(KC):
                        self.mm(ps[0:M, 0:n], wslab[:, kc, 0:M], hnT[:, kc, t0:t0 + n], kc == 0, kc == KC - 1,
                                [w_b, hn_b], [ps_b], inc=(kc == KC - 1))
                    evac(ps, ps_b, t0, n)

            with contextlib.ExitStack() as st2:
                wf, wf_b = self.sb(st2, "wf", [128, KC, 6], BF16)
                wf3, wf3_b = self.sb(st2, "wf3", [128, KC, 70], BF16)
                bfn, bfn_b = self.sb(st2, "bfn", [70, 1], F32)
                A, A_b = self.sb(st2, "fgA", [70, L], F32)
                E1, E1_b = self.sb(st2, "fgE", [70, L], F32)
                Z, Z_b = self.sb(st2, "fgZ", [70, L], F32)
                HI, HI_b = self.sb(st2, "fgHI", [70, L], BF16)
                self.wload(wf[:], w_in[:, :, O_FF:O_FF + 6], self.chwn(), wf_b)
                self.memset("dve", wf3[:], 0.0, [wf3_b])
                for r in (0, 32, 64):
                    self.cp("dve", wf3[:, :, r:r + 6], wf[:], [wf_b], [wf3_b])
                self.memset("dve", bfn[:], 0.0, [bfn_b])
                for r in (0, 32, 64):
                    self.dma("sp", bfn[r:r + 6, :], self.b_f[l].rearrange("(a b) -> a b", b=1), self.chn(), (), [bfn_b])
                self.ts("dve", bfn[:], bfn[:], -1.0, None, ALU.mult, None, [bfn_b], [bfn_b])
                self.memset("dve", Z[:], 0.0, [Z_b])

                def ev_f(ps, ps_b, t0, n):
                    self.act(E1[:, t0:t0 + n], ps[0:70, 0:n], AF.Exp, [ps_b, bfn_b], [E1_b], bias=bfn[:], scale=-1.0)
                proj_ws(wf3, wf3_b, 70, ev_f)
                self.act(E1[:], E1[:], AF.Ln, [E1_b], [E1_b], bias=self.epsc[1.0][0:70, :], scale=1.0)
                self.P.op("dve", lambda h: h.tensor_tensor_scan(A[:], E1[:], Z[:], 0.0, ALU.add, ALU.add),
                          [E1_b, Z_b], [A_b])
                ps, ps_b = self.psring.next()
                for ci, (t0, n) in enumerate(self.chunks):
                    self.tr(ps[0:n, ci * 6:ci * 6 + 6], A[0:6, t0:t0 + n], self.identf[0:6, 0:6],
                            [A_b, self.identf_b], [ps_b], inc=True)
                for ci, (t0, n) in enumerate(self.chunks):
                    self.cp("dve", cscol[0:n, ci, :], ps[0:n, ci * 6:ci * 6 + 6], [ps_b], [cs_b])
                self.ts("dve", E1[:], A[:], -1.0, None, ALU.mult, None, [A_b], [E1_b])
                self.cp("dve", HI[:], E1[:], [E1_b], [HI_b])
                self.cp("dve", C3[0:32, :], HI[0:32, :], [HI_b], [C3_b])
                self.tt("dve", Z[:], E1[:], HI[:], ALU.subtract, [E1_b, HI_b], [Z_b])
                self.cp("dve", HI[:], Z[:], [Z_b], [HI_b])
                self.cp("dve", C3[32:64, :], HI[32:64, :], [HI_b], [C3_b])
                self.tt("dve", Z[:], Z[:], HI[:], ALU.subtract, [HI_b, Z_b], [Z_b])
                self.cp("dve", C3[64:70, :], Z[64:70, :], [Z_b], [C3_b])
                P.barrier()

            with contextlib.ExitStack() as st2:
                wq_r = self.ring(st2, "wq", [128, KC, 128], BF16, 2)
                wk_r = self.ring(st2, "wk", [128, KC, 128], BF16, 2)
                wv_r = self.ring(st2, "wv", [128, KC, 128], BF16, 2)
                qT_r = self.ring(st2, "qT", [128, L], BF16, 2)
                kT_r = self.ring(st2, "kT", [128, L], BF16, 2)
                va_r = self.ring(st2, "vaug", [128, NCH, 132], BF16, 2)
                for (t, b) in va_r.items:
                    self.memset("dve", t[:], 1.0, [b])
                pt_r = self.ring(st2, "PT", [128, NCH, 512], BF16, 2)
                ytok_r = self.ring(st2, "ytok", [128, 128], BF16, 6)
                y1, y1_b = self.sb(st2, "y1", [128, NCH, 128], F32)
                ytmp_r = self.ring(st2, "ytmp", [128, 128], F32, 8)
                gsub, gsub_b = self.sb(st2, "gsub", [128, 128], F32)
                self.dma("sp", gsub[:], self.g_sub[l:l + 1, :].partition_broadcast(128), self.chn(), (), [gsub_b])
                lam_init = 0.8 - 0.6 * math.exp(-0.3 * l)
                self.ts("dve", gsub[:], gsub[:], 1.0 - lam_init, None, ALU.mult, None, [gsub_b], [gsub_b])
                cnt = [0]
                sring = Ring(self.ps[0:2])
                accring = Ring(self.ps[2:4])
                pring = Ring(self.ps[4:6])

                def proj_steps(colq, colk, colv, qscale):
                    wq, wq_b = wq_r.next()
                    wk, wk_b = wk_r.next()
                    wv, wv_b = wv_r.next()
                    qT, q_b = qT_r.next()
                    kT, k_b = kT_r.next()
                    va, v_b = va_r.next()
                    steps = []

                    def loads():
                        self.wload(wq[:], w_in[:, :, colq:colq + 128], self.chwn(), wq_b)
                        self.wload(wk[:], w_in[:, :, colk:colk + 128], self.chwn(), wk_b)
                        self.wload(wv[:], w_in[:, :, colv:colv + 128], self.chwn(), wv_b)
                    steps.append(loads)

                    def ws(w, w_b, dst, dst_b, t0, n, eng, scale):
                        ps, ps_b = pring.next()
                        for kc in range(KC):
                            self.mm(ps[:, 0:n], w[:, kc, :], hnT[:, kc, t0:t0 + n], kc == 0, kc == KC - 1,
                                    [w_b, hn_b], [ps_b], inc=(kc == KC - 1))
                        if eng == "act":
                            self.P.op("act", lambda h: h.activation(dst[:, t0:t0 + n], ps[:, 0:n], AF.Copy, scale=scale),
                                      [ps_b], [dst_b])
                        else:
                            self.cp("dve", dst[:, t0:t0 + n], ps[:, 0:n], [ps_b], [dst_b])

                    for (t0, n) in self.wtiles:
                        steps.append(lambda t0=t0, n=n: ws(wq, wq_b, qT, q_b, t0, n, "act", qscale))
                    for (t0, n) in self.wtiles:
                        steps.append(lambda t0=t0, n=n: ws(wk, wk_b, kT, k_b, t0, n, "dve", None))

                    def vs(ci, t0, n):
                        ps, ps_b = pring.next()
                        for kc in range(KC):
                            self.mm(ps[0:n, 0:128], hnT[:, kc, t0:t0 + n], wv[:, kc, :], kc == 0, kc == KC - 1,
                                    [wv_b, hn_b], [ps_b], inc=(kc == KC - 1))
                        cnt[0] += 1
                        self.cp("act" if cnt[0] % 2 else "dve", va[0:n, ci, 0:128], ps[0:n, 0:128], [ps_b], [v_b])
                    for ci, (t0, n) in enumerate(self.chunks):
                        steps.append(lambda ci=ci, t0=t0, n=n: vs(ci, t0, n))
                    return steps, (qT, q_b, kT, k_b, va, v_b)

                def emit_T(ytok, ytok_b, n, t0, ys, ys_b):
                    def d():
                        pt, pt_b = self.ptring.next()
                        self.tr(pt[:, 0, 0:n], ytok[0:n, :], self.ident[0:n, 0:n], [ytok_b, self.ident_b], [pt_b])
                        self.cp("act", ys[:, t0:t0 + n], pt[:, 0, 0:n], [pt_b], [ys_b])
                    return d

                def fox_attn(hh, hd):
                    qT, q_b, kT, k_b, va, v_b = hd
                    ys, ys_b = ystage.next()

                    def fox_out(ci, t0, n, acc, acc_b):
                        sm, sm_b = smr.next()
                        self.P.op("dve", lambda h: h.reciprocal(sm[0:n, 0:1], acc[0:n, 128:129]), [acc_b], [sm_b])
                        yt, yt_b = ytok_r.next()
                        self.ts("dve", yt[0:n, :], acc[0:n, 0:128], sm[0:n, 0:1], None, ALU.mult, None,
                                [acc_b, sm_b], [yt_b])
                        return [emit_T(yt, yt_b, n, t0, ys, ys_b)]
                    steps = self.attn_steps(qT, q_b, kT, k_b, 0, 128, va, v_b, pt_r, hh, cscol, cs_b, C3, C3_b, fox_out,
                                            sring, accring)
                    steps.append(lambda: self.dma("sp", self.yT_scr[s, :, hh, :], ys[:], self.chn(), [ys_b],
                                                  [self.b_yT[s]]))
                    return steps

                def diff_attn(hh, hd):
                    qT, q_b, kT, k_b, va, v_b = hd
                    ys, ys_b = ystage.next()

                    def d1(ci, t0, n, acc, acc_b):
                        sm, sm_b = smr.next()
                        self.P.op("dve", lambda h: h.reciprocal(sm[0:n, 0:1], acc[0:n, 128:129]), [acc_b], [sm_b])
                        self.ts("dve", y1[0:n, ci, :], acc[0:n, 0:128], sm[0:n, 0:1], None, ALU.mult, None,
                                [acc_b, sm_b], [y1_b])
                        return None

                    def d2(ci, t0, n, acc, acc_b):
                        sm, sm_b = smr.next()
                        self.P.op("dve", lambda h: h.reciprocal(sm[0:n, 0:1], acc[0:n, 128:129]), [acc_b], [sm_b])
                        yy, yy_b = ytmp_r.next()
                        y2, y2_b = ytmp_r.next()
                        self.ts("dve", yy[0:n, :], acc[0:n, 0:128], sm[0:n, 0:1], None, ALU.mult, None,
                                [acc_b, sm_b], [yy_b])
                        yt, yt_b = ytok_r.next()

                        def tail():
                            self.stt(yy[0:n, :], yy[0:n, :], self.neglam[l][0:n, :], y1[0:n, ci, :], ALU.mult, ALU.add,
                                     [yy_b, y1_b, self.lam_b], [yy_b])
                            self.stt(y2[0:n, :], yy[0:n, :], 1.0, yy[0:n, :], ALU.mult, ALU.mult, [yy_b], [y2_b, sm_b],
                                     accum=sm[0:n, 1:2])
                            self.rstd(sm[0:n, 3:4], sm[0:n, 1:2], 1.0 / 128, SUBLN_EPS, sm_b, sm_b, sm[0:n, 2:3], sm_b)
                            self.stt(yt[0:n, :], yy[0:n, :], sm[0:n, 3:4], gsub[0:n, :], ALU.mult, ALU.mult,
                                     [yy_b, sm_b, gsub_b], [yt_b])
                        return [tail, emit_T(yt, yt_b, n, t0, ys, ys_b)]
                    st1 = self.attn_steps(qT, q_b, kT, k_b, 0, 64, va, v_b, pt_r, None, None, None, None, None, d1,
                                          sring, accring)
                    st2_ = self.attn_steps(qT, q_b, kT, k_b, 64, 64, va, v_b, pt_r, None, None, None, None, None, d2,
                                           sring, accring)
                    steps = []
                    for a_, b_ in zip(st1, st2_):
                        steps.append(a_)
                        steps.append(b_)
                    steps.append(lambda: self.dma("sp", self.yT_scr[s, :, 6 + hh, :], ys[:], self.chn(), [ys_b],
                                                  [self.b_yT[s]]))
                    return steps

                heads = [("fox", hh) for hh in range(6)] + [("diff", hh) for hh in range(4)]

                def pj(kind, hh):
                    if kind == "fox":
                        return proj_steps(O_FQ + hh * 128, O_FK + hh * 128, O_FV + hh * 128, FOX_SCALE)
                    return proj_steps(O_DQ + hh * 128, O_DK + hh * 128, O_DV + hh * 128, DIFF_SCALE)

                psteps, hd = pj(*heads[0])
                for st_ in psteps:
                    st_()
                for i, (kind, hh) in enumerate(heads):
                    asteps = fox_attn(hh, hd) if kind == "fox" else diff_attn(hh, hd)
                    if i + 1 < len(heads):
                        psteps, hd_next = pj(*heads[i + 1])
                    else:
                        psteps, hd_next = [], None
                    self.run_merged(asteps, psteps)
                    hd = hd_next
                P.barrier()

            with contextlib.ExitStack() as st2:
                wa_r = self.ring(st2, "wa", [128, KC, 128], BF16, 2)
                wg_r = self.ring(st2, "wgt", [128, KC, 128], BF16, 2)
                U, U_b = self.sb(st2, "U", [128, 6, 30 + L], BF16)
                sig_r = self.ring(st2, "sig", [128, 512], F32, 3)
                self.memset("dve", U[:], 0.0, [U_b])
                for c in range(6):
                    wa, wa_b = wa_r.next()
                    wg, wg_b = wg_r.next()
                    self.wload(wa[:], w_in[:, :, O_CG + c * 128:O_CG + (c + 1) * 128], self.chwn(), wa_b)
                    self.wload(wg[:], w_in[:, :, O_CG + 768 + c * 128:O_CG + 768 + (c + 1) * 128], self.chwn(), wg_b)
                    for (t0, n) in self.wtiles:
                        pa, pa_b = self.psring.next()
                        pg, pg_b = self.psring.next()
                        for kc in range(KC):
                            self.mm(pg[:, 0:n], wg[:, kc, :], hnT[:, kc, t0:t0 + n], kc == 0, kc == KC - 1,
                                    [wg_b, hn_b], [pg_b], inc=(kc == KC - 1))
                        sg, sg_b = sig_r.next()
                        self.act(sg[:, 0:n], pg[:, 0:n], AF.Sigmoid, [pg_b], [sg_b])
                        for kc in range(KC):
                            self.mm(pa[:, 0:n], wa[:, kc, :], hnT[:, kc, t0:t0 + n], kc == 0, kc == KC - 1,
                                    [wa_b, hn_b], [pa_b], inc=(kc == KC - 1))
                        self.tt("dve", U[:, c, 30 + t0:30 + t0 + n], pa[:, 0:n], sg[:, 0:n], ALU.mult,
                                [pa_b, sg_b], [U_b])
                self.dma("sp", self.U_scr[s], U[:], self.chn(), [U_b], [self.b_U[s]])
                P.barrier()
            sth.close()
            stwo = contextlib.ExitStack()
            if self.fuse_wout:
                wo, wo_bs = self.wout_weights(stwo, l)

            with contextlib.ExitStack() as st2:
                U, U_b = self.sb(st2, "U2", [128, 6, 30 + L], BF16)
                self.dma("sp", U[:], self.U_scr[s], self.chn(), [self.b_U[s]], [U_b])
                UC_r = self.ring(st2, "UC", [128, 6, 512], F32, 2)
                sq_r = self.ring(st2, "sq", [128, 512], F32, 3)
                Dg, _ = self.sb(st2, "Dg", [128, 6, CONVW, 128], BF16)
                Dg_b = [[Buf("dg") for _ in range(CONVW)] for _ in range(6)]
                wdw, wdw_b = self.sb(st2, "wdw", [CONVW, 768], F32)
                wdT, wdT_b = self.sb(st2, "wdT", [128, 6, 32], F32)
                prm, prm_b = self.sb(st2, "cprm", [128, 3, 6], F32)
                prow, prow_b = self.sb(st2, "cprow", [18, 128], F32)
                stat_r = self.ring(st2, "cstat", [128, 512], F32, 4)
                self.dma("sp", wdw[:], self.w_dw[l], self.chn(), (), [wdw_b])
                ps, ps_b = self.psring.next()
                for c in range(6):
                    self.tr(ps[:, c * 32:c * 32 + CONVW], wdw[0:CONVW, c * 128:(c + 1) * 128],
                            self.identf[0:CONVW, 0:CONVW], [wdw_b, self.identf_b], [ps_b])
                for c in range(6):
                    self.cp("dve", wdT[:, c, 0:CONVW], ps[:, c * 32:c * 32 + CONVW], [ps_b], [wdT_b])
                for i, src in enumerate((self.b_dw, self.ln_g, self.ln_b)):
                    self.dma("sp", prow[i * 6:(i + 1) * 6, :], src[l].rearrange("(c p) -> c p", p=128), self.chn(), (),
                             [prow_b])
                ps, ps_b = self.psring.next()
                self.tr(ps[:, 0:18], prow[0:18, :], self.identf[0:18, 0:18], [prow_b, self.identf_b], [ps_b])
                self.cp("dve", prm[:].rearrange("p a c -> p (a c)"), ps[:, 0:18], [ps_b], [prm_b])
                for c in range(6):
                    for k in range(CONVW):
                        if k % 2:
                            self.P.op("act", lambda h, k=k, c=c: h.activation(
                                Dg[:, c, k, :], self.ident[:], AF.Copy, scale=wdT[:, c, k:k + 1]),
                                [self.ident_b, wdT_b], [Dg_b[c][k]])
                        else:
                            self.ts("dve", Dg[:, c, k, :], self.ident[:], wdT[:, c, k:k + 1], None,
                                    ALU.mult, None, [self.ident_b, wdT_b], [Dg_b[c][k]])

                def ln_tile(t0, n, UC, UC_b):
                    p1, p1_b = self.psring.next()
                    p2, p2_b = self.psring.next()
                    for c in range(6):
                        sq, sq_b = sq_r.next()
                        self.act(sq[:, 0:n], UC[:, c, 0:n], AF.Square, [UC_b], [sq_b])
                        self.mm(p1[:, 0:n], self.onesf[:], UC[:, c, 0:n], c == 0, c == 5, [self.onesf_b, UC_b], [p1_b],
                                inc=(c == 5))
                        self.mm(p2[:, 0:n], self.onesf[:], sq[:, 0:n], c == 0, c == 5, [self.onesf_b, sq_b], [p2_b],
                                inc=True)
                    mean, mean_b = stat_r.next()
                    var, var_b = stat_r.next()
                    self.ts("dve", mean[:, 0:n], p1[:, 0:n], 1.0 / 768, None, ALU.mult, None, [p1_b], [mean_b])
                    self.tt("dve", var[:, 0:n], mean[:, 0:n], mean[:, 0:n], ALU.mult, [mean_b], [var_b])
                    self.stt(var[:, 0:n], p2[:, 0:n], 1.0 / 768, var[:, 0:n], ALU.mult, ALU.subtract,
                             [p2_b, var_b], [var_b])
                    self.act(var[:, 0:n], var[:, 0:n], AF.Ln, [var_b], [var_b], bias=self.epsc[EPS][:, :], scale=1.0)
                    self.act(var[:, 0:n], var[:, 0:n], AF.Exp, [var_b], [var_b], scale=-0.5)
                    for c in range(6):
                        ys, ys_b = ystage.next()
                        self.tt("dve", UC[:, c, 0:n], UC[:, c, 0:n], mean[:, 0:n], ALU.subtract, [mean_b, UC_b], [UC_b])
                        self.tt("dve", UC[:, c, 0:n], UC[:, c, 0:n], var[:, 0:n], ALU.mult, [var_b, UC_b], [UC_b])
                        self.ts("dve", UC[:, c, 0:n], UC[:, c, 0:n], prm[:, 1, c:c + 1], prm[:, 2, c:c + 1], ALU.mult,
                                ALU.add, [prm_b, UC_b], [UC_b])
                        self.act(ys[:, 0:n], UC[:, c, 0:n], AF.Silu, [UC_b], [ys_b])
                        self.dma("sp", self.yT_scr[s, :, 10 + c, t0:t0 + n], ys[:, 0:n], self.chn(), [ys_b],
                                 [self.b_yT[s]])

                prev = None
                for (t0, n) in self.wtiles:
                    UC, UC_b = UC_r.next()
                    for c in range(6):
                        pc, pc_b = self.psring.next()
                        for k in range(CONVW):
                            self.mm(pc[:, 0:n], Dg[:, c, k, :], U[:, c, t0 + k:t0 + k + n], k == 0, k == CONVW - 1,
                                    [Dg_b[c][k], U_b], [pc_b], inc=(k == CONVW - 1))
                        self.ts("dve", UC[:, c, 0:n], pc[:, 0:n], prm[:, 0, c:c + 1], None, ALU.add, None,
                                [pc_b, prm_b], [UC_b])
                    if prev is not None:
                        ln_tile(*prev)
                    prev = (t0, n, UC, UC_b)
                ln_tile(*prev)
                P.barrier()
            if self.fuse_wout:
                self.wout_body(s, l, wo, wo_bs)
            stwo.close()
        P.barrier()

    def wout_weights(self, st, l):
        wo, _ = self.sb(st, "wo", [128, KC, D], BF16)
        wsrc = self.w_out[l].rearrange("(kc p) n -> p kc n", p=128)
        wo_bs = [Buf("wo%d" % i) for i in range(4)]
        for pn in range(4):
            self.wload(wo[:, :, pn * 512:(pn + 1) * 512], wsrc[:, :, pn * 512:(pn + 1) * 512], self.chwn(), wo_bs[pn])
        return wo, wo_bs

    def phase_wout(self, s, l):
        if self.fuse_wout:
            return
        with contextlib.ExitStack() as st:
            wo, wo_bs = self.wout_weights(st, l)
            self.wout_body(s, l, wo, wo_bs)

    def wout_body(self, s, l, wo, wo_bs):
        P = self.P
        L = self.L
        with contextlib.ExitStack() as st:
            yin_r = self.ring(st, "yin", [128, KC, 128], BF16, 3)
            hring = self.ring(st, "wh", [128, D], F32, 3)
            tring = self.ring(st, "wt", [128, D], F32, 2)
            yring = self.ring(st, "wy", [128, D], BF16, 4)
            stg = self.ring(st, "wstg", [128, KC, 128], BF16, 2)
            junk, junk_b = self.sb(st, "wjunk", [128, D], BF16)
            smr = self.ring(st, "wsm", [128, 16], F32, 4)
            gpost, gpost_b = self.gload(st, "g_post_mix", l)
            gpre, gpre_b = self.gload(st, "g_pre_ffn", l)
            def pre_w(ci):
                t0, n = self.chunks[ci]
                yin, yin_b = yin_r.next()
                nl_ = max(n, 128)
                self.dma("sp", yin[:, :, 0:nl_], self.yT_scr[s, :, :, t0:t0 + nl_], self.chn(), [self.b_yT[s]], [yin_b])
                h, h_b = hring.next()
                self.dma("sp", h[0:n, :], self.hres[s, t0:t0 + n, :], self.chn(), [self.b_hres[s]], [h_b])
                return yin, yin_b, h, h_b
            nxt = pre_w(0)
            pend = []
            for ci, (t0, n) in enumerate(self.chunks):
                yin, yin_b, h, h_b = nxt
                if ci + 1 < self.NCH:
                    nxt = pre_w(ci + 1)
                sm, sm_b = smr.next()
                t, t_b = tring.next()
                banks = []
                for pn in range(4):
                    if pn == 2 and len(pend) >= 2:
                        pend.pop(0)()
                    ps, ps_b = self.psring.next()
                    banks.append((ps, ps_b))
                    for kc in range(KC):
                        self.mm(ps[0:n, :], yin[:, kc, 0:n], wo[:, kc, pn * 512:(pn + 1) * 512], kc == 0, kc == KC - 1,
                                [yin_b, wo_bs[pn]], [ps_b], inc=(kc == KC - 1))
                    self.cp("act" if pn % 2 else "dve", t[0:n, pn * 512:(pn + 1) * 512], ps[0:n, 0:512], [ps_b], [t_b])
                self.act(junk[0:n, :], t[0:n, :], AF.Square, [t_b], [junk_b, sm_b], accum=sm[0:n, 0:1])
                self.rstd(sm[0:n, 2:3], sm[0:n, 0:1], 1.0 / D, EPS, sm_b, sm_b, sm[0:n, 1:2], sm_b)
                self.stt(t[0:n, :], t[0:n, :], sm[0:n, 2:3], gpost[0:n, :], ALU.mult, ALU.mult, [sm_b, gpost_b, t_b], [t_b])
                self.tt("dve", h[0:n, :], h[0:n, :], t[0:n, :], ALU.add, [t_b, h_b], [h_b])
                self.dma("sp", self.hres[s, t0:t0 + n, :], h[0:n, :], self.chn(), [h_b], [self.b_hres[s]])
                pend.append(self.prenorm_T(h, h_b, n, t0, gpre, gpre_b, junk, junk_b, sm, sm_b, yring, stg,
                                           self.hnT_scr[s], self.b_hnT[s], defer=True))
            while pend:
                pend.pop(0)()
        P.barrier()

    def phase_ffn1(self, s, l):
        P = self.P
        L = self.L
        wg_src = self.w_gate[l].rearrange("(kc p) n -> p kc n", p=128)
        wu_src = self.w_up[l].rearrange("(kc p) n -> p kc n", p=128)
        with contextlib.ExitStack() as st:
            hnT, _ = self.sb(st, "hnTf", [128, KC, L], BF16)
            hn_bs = {}
            for (t0, n) in self.wtiles:
                hn_bs[t0] = Buf("hnTf%d" % t0)
                self.dma("sp", hnT[:, :, t0:t0 + n], self.hnT_scr[s, :, :, t0:t0 + n], self.chn(), [self.b_hnT[s]],
                         [hn_bs[t0]])
            wg_r = self.ring(st, "fwg", [128, KC, 512], BF16, 2)
            wu_r = self.ring(st, "fwu", [128, KC, 512], BF16, 2)
            wcT_r = self.ring(st, "fwcT", [128, 4, 4], F32, 2)
            wcr_r = self.ring(st, "fwcr", [3, 512], F32, 2)
            G_r = self.ring(st, "fG", [128, 2 + L], BF16, 2)
            for (t_, b_) in G_r.items:
                self.memset("dve", t_[:], 0.0, [b_])
            cv_r = self.ring(st, "fcv", [128, 512], F32, 3)
            sl_r = self.ring(st, "fsl", [128, 512], F32, 3)
            a_r = self.ring(st, "faT4", [128, self.NCH, 4, 128], BF16, 2)
            for (t_, b_) in a_r.items:
                self.memset("dve", t_[:, 0, :, :], 0.0, [b_])
            aT_dst = self.aT_scr[s].rearrange("c p f t -> p c f t")
            fpend = [None]
            wc_state = {}

            def wc_load(g):
                wcr, wcr_b = wcr_r.next()
                self.dma("sp", wcr[:], self.w_fc[l][:, g * 512:(g + 1) * 512], self.chn(), (), [wcr_b])
                wc_state["r"] = (wcr, wcr_b)

            def wc_T():
                wcr, wcr_b = wc_state["r"]
                wcT, wcT_b = wcT_r.next()
                pw, pw_b = self.ptfring.next()
                for q in range(4):
                    self.tr(pw[:, q * 4:q * 4 + 3], wcr[0:3, q * 128:(q + 1) * 128], self.identf[0:3, 0:3],
                            [wcr_b, self.identf_b], [pw_b])
                for q in range(4):
                    self.cp("dve", wcT[:, q, 0:3], pw[:, q * 4:q * 4 + 3], [pw_b], [wcT_b])
                wc_state["T"] = (wcT, wcT_b)
            for g in range(FCH // 4):
                wg, wg_b = wg_r.next()
                wu, wu_b = wu_r.next()
                aT, aT_b = a_r.next()
                if g == 0:
                    wg_bl = [Buf("wg0%d" % q) for q in range(4)]
                    wu_bl = [Buf("wu0%d" % q) for q in range(4)]
                    for q in range(4):
                        self.wload(wg[:, :, q * 128:(q + 1) * 128], wg_src[:, :, q * 128:(q + 1) * 128], self.chwn(),
                                   wg_bl[q])
                        self.wload(wu[:, :, q * 128:(q + 1) * 128], wu_src[:, :, q * 128:(q + 1) * 128], self.chwn(),
                                   wu_bl[q])
                    wg_bl = [[b_, wg_b] for b_ in wg_bl]
                    wu_bl = [[b_, wu_b] for b_ in wu_bl]
                else:
                    self.wload(wg[:], wg_src[:, :, g * 512:(g + 1) * 512], self.chwn(), wg_b)
                    self.wload(wu[:], wu_src[:, :, g * 512:(g + 1) * 512], self.chwn(), wu_b)
                    wg_bl = [[wg_b]] * 4
                    wu_bl = [[wu_b]] * 4
                if g == 0:
                    wc_load(0)
                    wc_T()
                wcT, wcT_b = wc_state["T"]
                if g + 1 < FCH // 4:
                    wc_load(g + 1)
                for q in range(4):
                    fc = g * 4 + q
                    G, G_b = G_r.next()
                    for (t0, n) in self.wtiles:
                        pg, pg_b = self.psring.next()
                        pu, pu_b = self.psring.next()
                        for kc in range(KC):
                            self.mm(pg[:, 0:n], wg[:, kc, q * 128:(q + 1) * 128], hnT[:, kc, t0:t0 + n], kc == 0,
                                    kc == KC - 1, wg_bl[q] + [hn_bs[t0]], [pg_b], inc=(kc == KC - 1))
                        self.cp("act", G[:, 2 + t0:2 + t0 + n], pg[:, 0:n], [pg_b], [G_b])
                        for kc in range(KC):
                            self.mm(pu[:, 0:n], wu[:, kc, q * 128:(q + 1) * 128], hnT[:, kc, t0:t0 + n], kc == 0,
                                    kc == KC - 1, wu_bl[q] + [hn_bs[t0]], [pu_b], inc=(kc == KC - 1))
                        cv, cv_b = cv_r.next()
                        self.P.op("act", lambda h, cv=cv, pg=pg, n=n, wcT=wcT, q=q: h.activation(
                            cv[:, 0:n], pg[:, 0:n], AF.Copy, scale=wcT[:, q, 2:3]), [pg_b, wcT_b], [cv_b])
                        self.stt(cv[:, 0:n], G[:, t0 + 1:t0 + 1 + n], wcT[:, q, 1:2], cv[:, 0:n], ALU.mult, ALU.add,
                                 [G_b, wcT_b, cv_b], [cv_b])
                        self.stt(cv[:, 0:n], G[:, t0:t0 + n], wcT[:, q, 0:1], cv[:, 0:n], ALU.mult, ALU.add,
                                 [G_b, wcT_b, cv_b], [cv_b])
                        sl, sl_b = sl_r.next()
                        self.act(sl[:, 0:n], cv[:, 0:n], AF.Silu, [cv_b], [sl_b])
                        if fpend[0] is not None:
                            fpend[0]()

                        def fmul(aT=aT, aT_b=aT_b, q=q, t0=t0, n=n, pu=pu, pu_b=pu_b, sl=sl, sl_b=sl_b):
                            if n == 16:
                                self.tt("dve", aT[:, 0, q, 0:16], pu[:, 0:16], sl[:, 0:16], ALU.mult, [pu_b, sl_b], [aT_b])
                            else:
                                c0 = 1 + (t0 - 16) // 128
                                self.tt("dve", aT[:, c0:c0 + 4, q, :], pu[:, 0:512].rearrange("p (c t) -> p c t", t=128),
                                        sl[:, 0:512].rearrange("p (c t) -> p c t", t=128), ALU.mult, [pu_b, sl_b], [aT_b])
                        fpend[0] = fmul
                if fpend[0] is not None:
                    fpend[0]()
                    fpend[0] = None
                self.dma("sp", aT_dst[:, :, g * 4:(g + 1) * 4, :], aT[:], self.chn(), [aT_b], [self.b_aT[s]])
                if g + 1 < FCH // 4:
                    wc_T()
        P.barrier()

    def down_steps(self, st, s, l, nain=3):
        wd_src = self.w_down[l].rearrange("(fc p) n -> p fc n", p=128)
        wd_r = self.ring(st, "wd", [128, FCH, 512], BF16, 2)
        ain_r = self.ring(st, "ain", [128, FCH, 128], BF16, nain)
        o_r = self.ring(st, "dout", [128, 512], F32, 3)
        state = {"nxt": None, "wd": None}

        def pre_d(ci):
            t0, n = self.chunks[ci]
            ain, ain_b = ain_r.next()
            self.dma("sp", ain[:], self.aT_scr[s, ci], self.chn(), [self.b_aT[s]], [ain_b])
            return ain, ain_b

        def step(pn, ci):
            t0, n = self.chunks[ci]
            if ci == 0:
                wd, wd_b = wd_r.next()
                for q in range(4):
                    self.wload(wd[:, q * 11:(q + 1) * 11, :], wd_src[:, q * 11:(q + 1) * 11, pn * 512:(pn + 1) * 512],
                               self.chwn(), wd_b)
                state["wd"] = (wd, wd_b)
                if pn == 0:
                    state["nxt"] = pre_d(0)
            wd, wd_b = state["wd"]
            ain, ain_b = state["nxt"]
            if ci + 1 < self.NCH:
                state["nxt"] = pre_d(ci + 1)
            elif pn < 3:
                state["nxt"] = pre_d(0)
            ps, ps_b = self.psring.next()
            for fc in range(FCH):
                self.mm(ps[0:n, :], ain[:, fc, 0:n], wd[:, fc, :], fc == 0, fc == FCH - 1, [ain_b, wd_b], [ps_b],
                        inc=(fc == FCH - 1))
            o, o_b = o_r.next()
            self.cp("act" if ci % 2 else "dve", o[0:n, :], ps[0:n, :], [ps_b], [o_b])
            self.dma("sp", self.fraw[s, t0:t0 + n, pn * 512:(pn + 1) * 512], o[0:n, :], self.chn(), [o_b],
                     [self.b_fraw[s]])
        return [(lambda pn=pn, ci=ci: step(pn, ci)) for pn in range(4) for ci in range(self.NCH)]

    def phase_down(self, s, l, norm=None):
        with contextlib.ExitStack() as st:
            if norm is None:
                for st_ in self.down_steps(st, s, l):
                    st_()
            else:
                dsteps = self.down_steps(st, s, l, nain=2)
                nsteps = self.norm_steps(st, norm[0], norm[1], norm[2], nbuf=2)
                self.run_merged(dsteps, nsteps)
        self.P.barrier()


_CACHE = {}


def _consts():
    ident = np.eye(128, dtype=np.float32)
    kk = np.arange(128)[:, None]
    qq = np.arange(128)[None, :]
    mask = np.where(kk <= qq, 0.0, NEG).astype(np.float32)
    sel3 = np.zeros((70, 6, 128), np.float32)
    for h in range(6):
        for r in (0, 32, 64):
            sel3[r + h, h, :] = 1.0
    return ident, mask, sel3.reshape(70, 768)


def run(inputs, NS, SEQ, n_cores, stop_after=None, debug=False, trace=False):
    b = Builder(NS, SEQ, stop_after=stop_after, debug=debug)
    nc = b.build()
    ident, mask, sel3 = _consts()
    x = np.ascontiguousarray(inputs["x"], dtype=np.float32)
    in_maps = []
    for c in range(n_cores):
        m = {k: np.ascontiguousarray(v, dtype=np.float32) for k, v in inputs.items() if k != "x"}
        m["x"] = np.ascontiguousarray(x[c * NS:(c + 1) * NS, :SEQ])
        m["c_ident"] = ident
        m["c_mask"] = mask
        m["c_sel3"] = sel3
        in_maps.append(m)
    res = run_bass_kernel_spmd(nc, in_maps, core_ids=list(range(n_cores)), trace=trace)
    return res, b


def kernel(**inputs):
    res, _ = run(inputs, 2, 2048, 8)
    return np.concatenate([r["out"] for r in res.results], axis=0).astype(np.float32)
```

```python
import contextlib
import math
import numpy as np
import concourse.bass as bass
import concourse.mybir as mybir
from concourse.bass_utils import run_bass_kernel_spmd

F32 = mybir.dt.float32
BF16 = mybir.dt.bfloat16
AF = mybir.ActivationFunctionType
ALU = mybir.AluOpType
AX = mybir.AxisListType

D = 2048
KC = 16
NIN = 5382
DFF = 5632
FCH = 44
O_FQ, O_FK, O_FV, O_FF, O_DQ, O_DK, O_DV, O_CG = 0, 768, 1536, 2304, 2310, 2822, 3334, 3846
CONVW = 31
EPS = 1e-6
SUBLN_EPS = 1e-5
FOX_SCALE = 128 ** -0.5
DIFF_SCALE = 64 ** -0.5
DEPTH = 2
NEG = -30000.0


class Tok:
    __slots__ = ("sem", "val")

    def __init__(self, sem, val):
        self.sem = sem
        self.val = val


class Buf:
    __slots__ = ("name", "w", "r")

    def __init__(self, name=""):
        self.name = name
        self.w = None
        self.r = []


class Chan:
    __slots__ = ("sem", "count", "last")

    def __init__(self, sem):
        self.sem = sem
        self.count = 0
        self.last = None


class Eng:
    def __init__(self, name, sem):
        self.name = name
        self.sem = sem
        self.cnt = 0
        self.seen = {}
        self.ops = []


class Prog:
    def __init__(self, nc, stack):
        self.nc = nc
        self.stack = stack
        self.nsem = 0
        self.E = {}
        for name in ("pe", "act", "dve", "pool", "sp"):
            self.E[name] = Eng(name, self.new_sem("e_" + name))
        self.chans = []
        self.nops = 0

    def new_sem(self, name):
        self.nsem += 1
        return self.stack.enter_context(self.nc.semaphore(name))

    def chan(self):
        c = Chan(self.new_sem("ch%d" % self.nsem))
        self.chans.append(c)
        return c

    def _deps(self, e, reads, writes, extra=()):
        deps = {}

        def need(t):
            if t is not None:
                k = id(t.sem)
                if k not in deps or deps[k][1] < t.val:
                    deps[k] = (t.sem, t.val)

        for b in reads:
            need(b.w)
        for b in writes:
            need(b.w)
            for t in b.r:
                need(t)
        for t in extra:
            need(t)
        waits = []
        for k, (sem, val) in deps.items():
            if e.seen.get(k, 0) < val:
                e.seen[k] = val
                waits.append((sem, val))
        return waits

    def op(self, eng, fn, reads=(), writes=(), inc=True):
        e = self.E[eng]
        waits = self._deps(e, reads, writes)
        if eng == "pe":
            waits = [(s, v) for (s, v) in waits if s is not e.sem]
        if inc:
            e.cnt += 1
        tok = Tok(e.sem, e.cnt if inc else e.cnt + 1)
        e.ops.append((waits, fn, (e.sem, 1) if inc else None))
        for b in reads:
            b.r.append(tok)
            if len(b.r) > 64:
                b.r = b.r[-48:] if False else b.r
        for b in writes:
            b.w = tok
            b.r = []
        self.nops += 1
        return tok

    def dma(self, eng, fn, ch, reads=(), writes=()):
        e = self.E[eng]
        waits = self._deps(e, reads, writes, extra=(ch.last,))
        ch.count += 16
        tok = Tok(ch.sem, ch.count)
        ch.last = tok
        e.ops.append((waits, fn, (ch.sem, 16)))
        for b in reads:
            b.r.append(tok)
        for b in writes:
            b.w = tok
            b.r = []
        self.nops += 1
        return tok

    def barrier(self):
        allw = []
        for en in self.E.values():
            if en.cnt > 0:
                allw.append((en.sem, en.cnt))
        for c in self.chans:
            if c.last is not None:
                allw.append((c.sem, c.count))
        for en in self.E.values():
            waits = []
            for (s, v) in allw:
                if en.seen.get(id(s), 0) < v:
                    en.seen[id(s)] = v
                    waits.append((s, v))
            if waits:
                en.ops.append((waits, None, None))

    def emit(self):
        nc = self.nc
        self.barrier()
        with nc.Block() as block:
            def mk(en):
                def body(h):
                    for waits, fn, inc in en.ops:
                        for (s, v) in waits:
                            h.wait_ge(s, v)
                        if fn is not None:
                            ins = fn(h)
                            if inc is not None:
                                ins.then_inc(inc[0], inc[1])
                return body
            block.tensor(mk(self.E["pe"]))
            block.scalar(mk(self.E["act"]))
            block.vector(mk(self.E["dve"]))
            block.gpsimd(mk(self.E["pool"]))
            block.sync(mk(self.E["sp"]))


class Ring:
    def __init__(self, items):
        self.items = items
        self.i = 0

    def next(self):
        it = self.items[self.i % len(self.items)]
        self.i += 1
        return it


class Builder:
    def __init__(self, NS, SEQ, depth=DEPTH, stop_after=None, debug=False):
        self.NS, self.SEQ, self.depth = NS, SEQ, depth
        self.debug = debug
        self.fuse_wout = (stop_after is None and NS == 2)
        self.L = 16 + SEQ
        self.stop_after = stop_after
        self.chunks = [(0, 16)] + [(16 + 128 * i, 128) for i in range(SEQ // 128)]
        self.wtiles = [(0, 16)] + [(16 + 512 * j, 512) for j in range(SEQ // 512)]
        self.NCH = len(self.chunks)
        self.nc = bass.Bass("TRN2", target_bir_lowering=False)

    def din(self, name, shape, dt=F32):
        return self.nc.dram_tensor(name, list(shape), dt, kind="ExternalInput").ap()

    def dscr(self, name, shape, dt):
        return self.nc.dram_tensor(name, list(shape), dt, kind="ExternalOutput" if self.debug else "Internal").ap()

    def sb(self, st, name, shape, dt):
        self._n = getattr(self, "_n", 0) + 1
        t = st.enter_context(self.nc.sbuf_tensor("%s_%d" % (name, self._n), list(shape), dt))
        return t, Buf(name)

    def ring(self, st, name, shape, dt, n):
        return Ring([self.sb(st, name + str(i), shape, dt) for i in range(n)])

    def mm(self, out, lhsT, rhs, start, stop, reads, writes, inc):
        self.P.op("pe", lambda h: h.matmul(out, lhsT=lhsT, rhs=rhs, start=start, stop=stop),
                  reads, writes, inc=inc)

    def tr(self, out, in_, ident, reads, writes, inc=True):
        self.P.op("pe", lambda h: h.transpose(out, in_, ident), reads, writes, inc=inc)

    def act(self, out, in_, func, reads, writes, bias=None, scale=None, accum=None):
        kw = {}
        if bias is not None:
            kw["bias"] = bias
        if scale is not None:
            kw["scale"] = scale
        if accum is not None:
            kw["accum_out"] = accum
        self.P.op("act", lambda h: h.activation(out, in_, func, **kw), reads, writes)

    def tt(self, eng, out, in0, in1, op, reads, writes):
        self.P.op(eng, lambda h: h.tensor_tensor(out, in0, in1, op), reads, writes)

    def ts(self, eng, out, in0, s1, s2, op0, op1, reads, writes):
        if s2 is None:
            self.P.op(eng, lambda h: h.tensor_scalar(out, in0, s1, None, op0), reads, writes)
        else:
            self.P.op(eng, lambda h: h.tensor_scalar(out, in0, s1, s2, op0, op1), reads, writes)

    def stt(self, out, in0, scalar, in1, op0, op1, reads, writes, accum=None):
        if accum is None:
            self.P.op("dve", lambda h: h.scalar_tensor_tensor(out, in0, scalar, in1, op0, op1), reads, writes)
        else:
            self.P.op("dve", lambda h: h.scalar_tensor_tensor(out, in0, scalar, in1, op0, op1, accum),
                      reads, writes)

    def cp(self, eng, out, in_, reads, writes):
        if eng == "act":
            self.P.op("act", lambda h: h.activation(out, in_, AF.Copy), reads, writes)
        else:
            self.P.op(eng, lambda h: h.tensor_copy(out, in_), reads, writes)

    def memset(self, eng, ap, val, writes):
        self.P.op(eng, lambda h: h.memset(ap, val), (), writes)

    def dma(self, q, out, in_, ch, reads, writes, **kw):
        return self.P.dma(q, lambda h: h.dma_start(out=out, in_=in_, **kw), ch, reads, writes)

    def wload(self, out, in_, ch, wbuf):
        return self.dma("pool", out, in_, ch, (), [wbuf])

    def rstd(self, out, ss, scale, eps, reads_b, out_b, tmp, tmp_b):
        n = ss.shape[0]
        self.act(tmp, ss, AF.Ln, [reads_b], [tmp_b], bias=self.epsc[eps][0:n, :], scale=scale)
        self.act(out, tmp, AF.Exp, [tmp_b], [out_b], scale=-0.5)

    def build(self):
        nc = self.nc
        NS, SEQ, L = self.NS, self.SEQ, self.L
        self.x = self.din("x", [NS, SEQ, D])
        self.meta = self.din("meta_tokens", [16, D])
        self.w_in = self.din("w_in", [DEPTH, D, NIN])
        self.b_f = self.din("b_f", [DEPTH, 6])
        self.lam = [self.din(n, [DEPTH, 64]) for n in ("lam_q1", "lam_k1", "lam_q2", "lam_k2")]
        self.g_sub = self.din("g_sub", [DEPTH, 128])
        self.w_dw = self.din("w_dw", [DEPTH, CONVW, 768])
        self.b_dw = self.din("b_dw", [DEPTH, 768])
        self.ln_g = self.din("ln_g", [DEPTH, 768])
        self.ln_b = self.din("ln_b", [DEPTH, 768])
        self.w_out = self.din("w_out", [DEPTH, D, D])
        self.w_gate = self.din("w_gate", [DEPTH, D, DFF])
        self.w_up = self.din("w_up", [DEPTH, D, DFF])
        self.w_fc = self.din("w_ffn_conv", [DEPTH, 3, DFF])
        self.w_down = self.din("w_down", [DEPTH, DFF, D])
        self.gn = {n: self.din(n, [DEPTH, D]) for n in ("g_pre_mix", "g_post_mix", "g_pre_ffn", "g_post_ffn")}
        self.c_ident = self.din("c_ident", [128, 128])
        self.c_mask = self.din("c_mask", [128, 128])
        self.c_sel3 = self.din("c_sel3", [70, 768])
        self.out = nc.dram_tensor("out", [NS, SEQ, D], F32, kind="ExternalOutput").ap()
        self.hres = self.dscr("hres", [NS, L, D], F32)
        self.fraw = self.dscr("fraw", [NS, L, D], F32)
        self.hnT_scr = self.dscr("hnT_scr", [NS, 128, KC, L], BF16)
        self.yT_scr = self.dscr("yT_scr", [NS, 128, KC, L], BF16)
        self.aT_scr = self.dscr("aT_scr", [NS, self.NCH, 128, FCH, 128], BF16)
        self.U_scr = self.dscr("U_scr", [NS, 128, 6, 30 + L], BF16)
        self.b_U = [Buf("Us") for _ in range(NS)]
        self.b_hres = [Buf("hres") for _ in range(NS)]
        self.b_fraw = [Buf("fraw") for _ in range(NS)]
        self.b_hnT = [Buf("hnTs") for _ in range(NS)]
        self.b_yT = [Buf("yTs") for _ in range(NS)]
        self.b_aT = [Buf("aTs") for _ in range(NS)]
        self.b_out = Buf("out")

        with contextlib.ExitStack() as st:
            self.P = P = Prog(nc, st)
            self.ps = []
            for i in range(6):
                t = st.enter_context(nc.psum_tensor("psb%d" % i, [128, 512], F32))
                self.ps.append((t, Buf("psb%d" % i)))
            self.ptr = []
            for i in range(2):
                t = st.enter_context(nc.psum_tensor("ptr%d" % i, [128, 8, 128], BF16))
                self.ptr.append((t, Buf("ptr%d" % i)))
            self.psring = Ring(self.ps)
            self.ptring = Ring(self.ptr)
            self.ptfring = Ring(self.ps[4:6])
            ch0 = P.chan()
            self.chs = [P.chan() for _ in range(8)]
            self.chw = [P.chan() for _ in range(6)]
            self._chi = 0
            self._chwi = 0
            tmpf, tmpf_b = self.sb(st, "tmpf", [128, 768], F32)
            self.ident, self.ident_b = self.sb(st, "ident", [128, 128], BF16)
            self.identf, self.identf_b = self.sb(st, "identf", [128, 128], F32)
            self.maskT, self.maskT_b = self.sb(st, "maskT", [128, 128], BF16)
            self.sel3, self.sel3_b = self.sb(st, "sel3", [70, 768], BF16)
            self.onesf, self.onesf_b = self.sb(st, "onesf", [128, 128], F32)
            self.dma("sp", self.identf[:], self.c_ident[:, :], ch0, (), [self.identf_b])
            self.cp("dve", self.ident[:], self.identf[:], [self.identf_b], [self.ident_b])
            self.dma("sp", tmpf[:, 0:128], self.c_mask[:, :], ch0, (), [tmpf_b])
            self.cp("dve", self.maskT[:], tmpf[:, 0:128], [tmpf_b], [self.maskT_b])
            self.dma("sp", tmpf[0:70, :], self.c_sel3[:, :], ch0, [], [tmpf_b])
            self.cp("dve", self.sel3[:], tmpf[0:70, :], [tmpf_b], [self.sel3_b])
            self.memset("dve", self.onesf[:], 1.0, [self.onesf_b])
            self.epsc = {}
            self.eps_b = Buf("eps")
            for e in (EPS, SUBLN_EPS, 1.0):
                t, _ = self.sb(st, "epsc", [128, 1], F32)
                self.memset("dve", t[:], e, [self.eps_b])
                self.epsc[e] = t[:]
            self.neglam = []
            self.lam_b = Buf("lam")
            lt, lt_b = self.sb(st, "lamt", [128, 4, 64], F32)
            lj, lj_b = self.sb(st, "lamj", [128, 64], F32)
            for l in range(self.depth):
                for i in range(4):
                    self.dma("sp", lt[:, i, :], self.lam[i][l:l + 1, :].partition_broadcast(128), ch0, (), [lt_b])
                s2, s2_b = self.sb(st, "lams", [128, 2], F32)
                self.stt(lj[:], lt[:, 0, :], 1.0, lt[:, 1, :], ALU.mult, ALU.mult, [lt_b], [lj_b, s2_b], accum=s2[:, 0:1])
                self.stt(lj[:], lt[:, 2, :], 1.0, lt[:, 3, :], ALU.mult, ALU.mult, [lt_b], [lj_b, s2_b], accum=s2[:, 1:2])
                e2, e2_b = self.sb(st, "lame", [128, 2], F32)
                self.act(e2[:], s2[:], AF.Exp, [s2_b], [e2_b])
                nl, _ = self.sb(st, "neglam", [128, 1], F32)
                lam_init = 0.8 - 0.6 * math.exp(-0.3 * l)
                self.tt("dve", nl[:], e2[:, 1:2], e2[:, 0:1], ALU.subtract, [e2_b], [self.lam_b])
                self.ts("dve", nl[:], nl[:], -lam_init, None, ALU.add, None, [self.lam_b], [self.lam_b])
                self.neglam.append(nl)
            P.barrier()

            self.schedule()
            P.emit()
        return nc

    def schedule(self):
        NS, depth = self.NS, self.depth
        stop = self.stop_after
        if stop is not None or NS != 2:
            for s in range(NS):
                self.phase_norm(s, 0, first=True)
                for l in range(depth):
                    for name, fn in (("mixer", self.phase_mixer), ("wout", self.phase_wout),
                                     ("ffn1", self.phase_ffn1), ("down", self.phase_down)):
                        fn(s, l)
                        if stop == (s, l, name):
                            return
                    self.phase_norm(s, l + 1, first=False)
                    if stop == (s, l, "norm"):
                        return
            return
        self.phase_norm(0, 0, first=True)
        for l in range(depth):
            for s in range(2):
                self.phase_mixer(s, l)
                self.phase_wout(s, l)
                self.phase_ffn1(s, l)
                if s == 0:
                    nrm = (1, 0, True) if l == 0 else (1, l, False)
                else:
                    nrm = (0, l + 1, False)
                self.phase_down(s, l, norm=nrm)
        self.phase_norm(1, depth, first=False)

    def chn(self):
        self._chi += 1
        return self.chs[self._chi % len(self.chs)]

    def chwn(self):
        self._chwi += 1
        return self.chw[self._chwi % len(self.chw)]

    def gload(self, st, name, l):
        t, b = self.sb(st, name, [128, D], F32)
        self.dma("sp", t[:], self.gn[name][l:l + 1, :].partition_broadcast(128), self.chn(), (), [b])
        return t, b

    def norm_steps(self, st, s, l, first, nbuf=3):
        last = (l == self.depth)
        hring = self.ring(st, "nh", [128, D], F32, nbuf)
        fring = self.ring(st, "nf", [128, D], F32, nbuf) if not first else None
        yring = self.ring(st, "ny", [128, D], BF16, 2)
        stg = self.ring(st, "nstg", [128, KC, 128], BF16, 2)
        junk, junk_b = self.sb(st, "njunk", [128, D], BF16)
        smr = self.ring(st, "nsm", [128, 8], F32, 3)
        gpost = gpost_b = gpre = gpre_b = None
        if not first:
            gpost, gpost_b = self.gload(st, "g_post_ffn", l - 1)
        if not last:
            gpre, gpre_b = self.gload(st, "g_pre_mix", l)

        def pre_n(ci):
            t0, n = self.chunks[ci]
            h, h_b = hring.next()
            f = f_b = None
            if first:
                src = self.meta[:, :] if ci == 0 else self.x[s, t0 - 16:t0 - 16 + n, :]
                self.dma("sp", h[0:n, :], src, self.chn(), (), [h_b])
            else:
                f, f_b = fring.next()
                self.dma("sp", h[0:n, :], self.hres[s, t0:t0 + n, :], self.chn(), [self.b_hres[s]], [h_b])
                self.dma("sp", f[0:n, :], self.fraw[s, t0:t0 + n, :], self.chn(), [self.b_fraw[s]], [f_b])
            return h, h_b, f, f_b
        state = {"nxt": None, "pend": None}

        def step(ci):
            t0, n = self.chunks[ci]
            if ci == 0:
                state["nxt"] = pre_n(0)
            h, h_b, f, f_b = state["nxt"]
            if ci + 1 < self.NCH:
                state["nxt"] = pre_n(ci + 1)
            if state["pend"] is not None:
                state["pend"]()
                state["pend"] = None
            sm, sm_b = smr.next()
            if not first:
                self.act(junk[0:n, :], f[0:n, :], AF.Square, [f_b], [junk_b, sm_b], accum=sm[0:n, 0:1])
                self.rstd(sm[0:n, 2:3], sm[0:n, 0:1], 1.0 / D, EPS, sm_b, sm_b, sm[0:n, 1:2], sm_b)
                self.stt(f[0:n, :], f[0:n, :], sm[0:n, 2:3], gpost[0:n, :], ALU.mult, ALU.mult,
                         [sm_b, gpost_b, f_b], [f_b])
                self.tt("dve", h[0:n, :], h[0:n, :], f[0:n, :], ALU.add, [f_b, h_b], [h_b])
            if last:
                if ci > 0:
                    self.dma("sp", self.out[s, t0 - 16:t0 - 16 + n, :], h[0:n, :], self.chn(), [h_b], [self.b_out])
                return
            self.dma("sp", self.hres[s, t0:t0 + n, :], h[0:n, :], self.chn(), [h_b], [self.b_hres[s]])
            state["pend"] = self.prenorm_T(h, h_b, n, t0, gpre, gpre_b, junk, junk_b, sm, sm_b, yring, stg,
                                           self.hnT_scr[s], self.b_hnT[s], defer=True)

        def fin():
            if state["pend"] is not None:
                state["pend"]()
                state["pend"] = None
        return [(lambda ci=ci: step(ci)) for ci in range(self.NCH)] + [fin]

    def phase_norm(self, s, l, first):
        with contextlib.ExitStack() as st:
            for st_ in self.norm_steps(st, s, l, first):
                st_()
        self.P.barrier()

    def prenorm_T(self, h, h_b, n, t0, g, g_b, junk, junk_b, sm, sm_b, yring, stg, dstT, dst_b, defer=False):
        y, y_b = yring.next()
        self.act(junk[0:n, :], h[0:n, :], AF.Square, [h_b], [junk_b, sm_b], accum=sm[0:n, 4:5])
        self.rstd(sm[0:n, 6:7], sm[0:n, 4:5], 1.0 / D, EPS, sm_b, sm_b, sm[0:n, 5:6], sm_b)
        self.stt(y[0:n, :], h[0:n, :], sm[0:n, 6:7], g[0:n, :], ALU.mult, ALU.mult, [sm_b, g_b, h_b], [y_b])
        def fin():
            sg, sg_b = stg.next()
            for half in range(2):
                pt, pt_b = self.ptring.next()
                for k in range(8):
                    kc = half * 8 + k
                    self.tr(pt[:, k, 0:n], y[0:n, kc * 128:(kc + 1) * 128], self.ident[0:n, 0:n],
                            [y_b, self.ident_b], [pt_b], inc=(k == 7))
                self.cp("act" if half == 0 else "dve", sg[:, half * 8:(half + 1) * 8, 0:n], pt[:, :, 0:n], [pt_b], [sg_b])
            self.dma("sp", dstT[:, :, t0:t0 + n], sg[:, :, 0:n], self.chn(), [sg_b], [dst_b])
        if defer:
            return fin
        fin()
        return None

    def attn_steps(self, qT, q_b, kT, k_b, p0, K, vaug, v_b, ptring, bias_h, cscol, cs_b, C3, C3_b, consumer,
                   sring, accring):
        steps = []
        pending = []

        def advance():
            for ent in list(pending):
                ent.pop(0)()
                if not ent:
                    pending.remove(ent)

        def flush():
            while pending:
                advance()

        for j, (q0, nq) in enumerate(self.wtiles):
            if j == 0:
                klist = [(0, 0, True)]
            else:
                klist = [(c, 0, False) for c in range(0, 4 * (j - 1) + 1)]
                klist += [(4 * (j - 1) + 1 + m, 128 * m, True) for m in range(4)]
            cell = {}

            def s1(c, col0, diag, first, j=j, q0=q0, nq=nq, cell=cell):
                if first:
                    cell["PT"] = ptring.next()
                PT, PT_b = cell["PT"]
                k0, nk = self.chunks[c]
                sps, sps_b = sring.next()
                last_plain = (bias_h is None) and (not diag)
                self.mm(sps[0:nk, col0:nq], kT[p0:p0 + K, k0:k0 + nk], qT[p0:p0 + K, q0 + col0:q0 + nq],
                        True, last_plain, [k_b, q_b], [sps_b], inc=last_plain)
                if bias_h is not None:
                    self.mm(sps[0:nk, col0:nq], self.sel3[0:70, bias_h * 128:bias_h * 128 + nk],
                            C3[0:70, q0 + col0:q0 + nq], False, not diag, [self.sel3_b, C3_b], [sps_b], inc=not diag)
                if diag:
                    self.mm(sps[0:nk, col0:col0 + nk], self.ident[0:nk, 0:nk], self.maskT[0:nk, 0:nk],
                            False, True, [self.ident_b, self.maskT_b], [sps_b], inc=True)
                if bias_h is not None:
                    self.act(PT[0:nk, c, col0:nq], sps[0:nk, col0:nq], AF.Exp, [sps_b, cs_b], [PT_b],
                             bias=cscol[0:nk, c, bias_h:bias_h + 1])
                else:
                    self.act(PT[0:nk, c, col0:nq], sps[0:nk, col0:nq], AF.Exp, [sps_b], [PT_b])
                advance()

            for i, (c, col0, diag) in enumerate(klist):
                steps.append(lambda c=c, col0=col0, diag=diag, first=(i == 0), s1=s1: s1(c, col0, diag, first))

            subs = [(0, 0)] if j == 0 else [(4 * (j - 1) + 1 + b, b) for b in range(4)]

            def s2(ci, b, cell=cell):
                PT, PT_b = cell["PT"]
                t0, n = self.chunks[ci]
                acc, acc_b = accring.next()
                for c in range(ci + 1):
                    k0, nk = self.chunks[c]
                    self.mm(acc[0:n, 0:129], PT[0:nk, c, b * 128:b * 128 + n], vaug[0:nk, c, 0:129],
                            c == 0, c == ci, [PT_b, v_b], [acc_b], inc=(c == ci))
                d = consumer(ci, t0, n, acc, acc_b)
                advance()
                if d:
                    pending.append(list(d))

            for (ci, b) in subs:
                steps.append(lambda ci=ci, b=b, s2=s2: s2(ci, b))
        steps.append(flush)
        return steps

    @staticmethod
    def run_merged(a_steps, b_steps):
        na, nb = len(a_steps), len(b_steps)
        bi = 0
        for i, st_ in enumerate(a_steps):
            st_()
            while bi < nb and (bi + 1) * na <= (i + 1) * nb:
                b_steps[bi]()
                bi += 1
        while bi < nb:
            b_steps[bi]()
            bi += 1

    def phase_mixer(self, s, l):
        P = self.P
        L, NCH = self.L, self.NCH
        w_in = self.w_in[l].rearrange("(kc p) n -> p kc n", p=128)
        with contextlib.ExitStack() as st:
            ystage = self.ring(st, "ystage", [128, L], BF16, 2)
            smr = self.ring(st, "msm", [128, 8], F32, 8)
            cscol, cs_b = self.sb(st, "cscol", [128, NCH, 6], F32)
            C3, C3_b = self.sb(st, "C3", [70, L], BF16)
            sth = contextlib.ExitStack()
            hnT, hn_b = self.sb(sth, "hnT", [128, KC, L], BF16)
            self.dma("sp", hnT[:], self.hnT_scr[s], self.chn(), [self.b_hnT[s]], [hn_b])

            def proj_ws(wslab, w_b, M, evac):
                for (t0, n) in self.wtiles:
                    ps, ps_b = self.psring.next()
                    for kc in range(KC):
                        self.mm(ps[0:M, 0:n], wslab[:, kc, 0:M], hnT[:, kc, t0:t0 + n], kc == 0, kc == KC - 1,
                                [w_b, hn_b], [ps_b], inc=(kc == KC - 1))
                    evac(ps, ps_b, t0, n)

            with contextlib.ExitStack() as st2:
                wf, wf_b = self.sb(st2, "wf", [128, KC, 6], BF16)
                wf3, wf3_b = self.sb(st2, "wf3", [128, KC, 70], BF16)
                bfn, bfn_b = self.sb(st2, "bfn", [70, 1], F32)
                A, A_b = self.sb(st2, "fgA", [70, L], F32)
                E1, E1_b = self.sb(st2, "fgE", [70, L], F32)
                Z, Z_b = self.sb(st2, "fgZ", [70, L], F32)
                HI, HI_b = self.sb(st2, "fgHI", [70, L], BF16)
                self.wload(wf[:], w_in[:, :, O_FF:O_FF + 6], self.chwn(), wf_b)
                self.memset("dve", wf3[:], 0.0, [wf3_b])
                for r in (0, 32, 64):
                    self.cp("dve", wf3[:, :, r:r + 6], wf[:], [wf_b], [wf3_b])
                self.memset("dve", bfn[:], 0.0, [bfn_b])
                for r in (0, 32, 64):
                    self.dma("sp", bfn[r:r + 6, :], self.b_f[l].rearrange("(a b) -> a b", b=1), self.chn(), (), [bfn_b])
                self.ts("dve", bfn[:], bfn[:], -1.0, None, ALU.mult, None, [bfn_b], [bfn_b])
                self.memset("dve", Z[:], 0.0, [Z_b])

                def ev_f(ps, ps_b, t0, n):
                    self.act(E1[:, t0:t0 + n], ps[0:70, 0:n], AF.Exp, [ps_b, bfn_b], [E1_b], bias=bfn[:], scale=-1.0)
                proj_ws(wf3, wf3_b, 70, ev_f)
                self.act(E1[:], E1[:], AF.Ln, [E1_b], [E1_b], bias=self.epsc[1.0][0:70, :], scale=1.0)
                self.P.op("dve", lambda h: h.tensor_tensor_scan(A[:], E1[:], Z[:], 0.0, ALU.add, ALU.add),
                          [E1_b, Z_b], [A_b])
                ps, ps_b = self.psring.next()
                for ci, (t0, n) in enumerate(self.chunks):
                    self.tr(ps[0:n, ci * 6:ci * 6 + 6], A[0:6, t0:t0 + n], self.identf[0:6, 0:6],
                            [A_b, self.identf_b], [ps_b], inc=True)
                for ci, (t0, n) in enumerate(self.chunks):
                    self.cp("dve", cscol[0:n, ci, :], ps[0:n, ci * 6:ci * 6 + 6], [ps_b], [cs_b])
                self.ts("dve", E1[:], A[:], -1.0, None, ALU.mult, None, [A_b], [E1_b])
                self.cp("dve", HI[:], E1[:], [E1_b], [HI_b])
                self.cp("dve", C3[0:32, :], HI[0:32, :], [HI_b], [C3_b])
                self.tt("dve", Z[:], E1[:], HI[:], ALU.subtract, [E1_b, HI_b], [Z_b])
                self.cp("dve", HI[:], Z[:], [Z_b], [HI_b])
                self.cp("dve", C3[32:64, :], HI[32:64, :], [HI_b], [C3_b])
                self.tt("dve", Z[:], Z[:], HI[:], ALU.subtract, [HI_b, Z_b], [Z_b])
                self.cp("dve", C3[64:70, :], Z[64:70, :], [Z_b], [C3_b])
                P.barrier()

            with contextlib.ExitStack() as st2:
                wq_r = self.ring(st2, "wq", [128, KC, 128], BF16, 2)
                wk_r = self.ring(st2, "wk", [128, KC, 128], BF16, 2)
                wv_r = self.ring(st2, "wv", [128, KC, 128], BF16, 2)
                qT_r = self.ring(st2, "qT", [128, L], BF16, 2)
                kT_r = self.ring(st2, "kT", [128, L], BF16, 2)
                va_r = self.ring(st2, "vaug", [128, NCH, 132], BF16, 2)
                for (t, b) in va_r.items:
                    self.memset("dve", t[:], 1.0, [b])
                pt_r = self.ring(st2, "PT", [128, NCH, 512], BF16, 2)
                ytok_r = self.ring(st2, "ytok", [128, 128], BF16, 6)
                y1, y1_b = self.sb(st2, "y1", [128, NCH, 128], F32)
                ytmp_r = self.ring(st2, "ytmp", [128, 128], F32, 8)
                gsub, gsub_b = self.sb(st2, "gsub", [128, 128], F32)
                self.dma("sp", gsub[:], self.g_sub[l:l + 1, :].partition_broadcast(128), self.chn(), (), [gsub_b])
                lam_init = 0.8 - 0.6 * math.exp(-0.3 * l)
                self.ts("dve", gsub[:], gsub[:], 1.0 - lam_init, None, ALU.mult, None, [gsub_b], [gsub_b])
                cnt = [0]
                sring = Ring(self.ps[0:3])
                accring = Ring(self.ps[3:5])
                ptr1f = self.ptr[1][0][:].rearrange("p a b -> p (a b)").bitcast(F32)
                pring = Ring([self.ps[5], (ptr1f, self.ptr[1][1])])
                att_tr = Ring([self.ptr[0]])

                def proj_steps(colq, colk, colv, qscale):
                    wq, wq_b = wq_r.next()
                    wk, wk_b = wk_r.next()
                    wv, wv_b = wv_r.next()
                    qT, q_b = qT_r.next()
                    kT, k_b = kT_r.next()
                    va, v_b = va_r.next()
                    steps = []

                    def loads():
                        self.wload(wq[:], w_in[:, :, colq:colq + 128], self.chwn(), wq_b)
                        self.wload(wk[:], w_in[:, :, colk:colk + 128], self.chwn(), wk_b)
                        self.wload(wv[:], w_in[:, :, colv:colv + 128], self.chwn(), wv_b)
                    steps.append(loads)

                    def ws(w, w_b, dst, dst_b, t0, n, eng, scale):
                        ps, ps_b = pring.next()
                        for kc in range(KC):
                            self.mm(ps[:, 0:n], w[:, kc, :], hnT[:, kc, t0:t0 + n], kc == 0, kc == KC - 1,
                                    [w_b, hn_b], [ps_b], inc=(kc == KC - 1))
                        if eng == "act":
                            self.P.op("act", lambda h: h.activation(dst[:, t0:t0 + n], ps[:, 0:n], AF.Copy, scale=scale),
                                      [ps_b], [dst_b])
                        else:
                            self.cp("dve", dst[:, t0:t0 + n], ps[:, 0:n], [ps_b], [dst_b])

                    for (t0, n) in self.wtiles:
                        steps.append(lambda t0=t0, n=n: ws(wq, wq_b, qT, q_b, t0, n, "act", qscale))
                    for (t0, n) in self.wtiles:
                        steps.append(lambda t0=t0, n=n: ws(wk, wk_b, kT, k_b, t0, n, "dve", None))

                    def vs(ci, t0, n):
                        ps, ps_b = pring.next()
                        for kc in range(KC):
                            self.mm(ps[0:n, 0:128], hnT[:, kc, t0:t0 + n], wv[:, kc, :], kc == 0, kc == KC - 1,
                                    [wv_b, hn_b], [ps_b], inc=(kc == KC - 1))
                        cnt[0] += 1
                        self.cp("act" if cnt[0] % 2 else "dve", va[0:n, ci, 0:128], ps[0:n, 0:128], [ps_b], [v_b])
                    for ci, (t0, n) in enumerate(self.chunks):
                        steps.append(lambda ci=ci, t0=t0, n=n: vs(ci, t0, n))
                    return steps, (qT, q_b, kT, k_b, va, v_b)

                def emit_T(ytok, ytok_b, n, t0, ys, ys_b):
                    def d():
                        pt, pt_b = att_tr.next()
                        k_ = att_tr.i % 8
                        self.tr(pt[:, k_, 0:n], ytok[0:n, :], self.ident[0:n, 0:n], [ytok_b, self.ident_b], [pt_b])
                        self.cp("act", ys[:, t0:t0 + n], pt[:, k_, 0:n], [pt_b], [ys_b])
                    return d

                def fox_attn(hh, hd):
                    qT, q_b, kT, k_b, va, v_b = hd
                    ys, ys_b = ystage.next()

                    def fox_out(ci, t0, n, acc, acc_b):
                        sm, sm_b = smr.next()
                        self.P.op("dve", lambda h: h.reciprocal(sm[0:n, 0:1], acc[0:n, 128:129]), [acc_b], [sm_b])
                        yt, yt_b = ytok_r.next()
                        self.ts("dve", yt[0:n, :], acc[0:n, 0:128], sm[0:n, 0:1], None, ALU.mult, None,
                                [acc_b, sm_b], [yt_b])
                        return [emit_T(yt, yt_b, n, t0, ys, ys_b)]
                    steps = self.attn_steps(qT, q_b, kT, k_b, 0, 128, va, v_b, pt_r, hh, cscol, cs_b, C3, C3_b, fox_out,
                                            sring, accring)
                    steps.append(lambda: self.dma("sp", self.yT_scr[s, :, hh, :], ys[:], self.chn(), [ys_b],
                                                  [self.b_yT[s]]))
                    return steps

                def diff_attn(hh, hd):
                    qT, q_b, kT, k_b, va, v_b = hd
                    ys, ys_b = ystage.next()

                    def d1(ci, t0, n, acc, acc_b):
                        sm, sm_b = smr.next()
                        self.P.op("dve", lambda h: h.reciprocal(sm[0:n, 0:1], acc[0:n, 128:129]), [acc_b], [sm_b])
                        self.ts("dve", y1[0:n, ci, :], acc[0:n, 0:128], sm[0:n, 0:1], None, ALU.mult, None,
                                [acc_b, sm_b], [y1_b])
                        return None

                    def d2(ci, t0, n, acc, acc_b):
                        sm, sm_b = smr.next()
                        self.P.op("dve", lambda h: h.reciprocal(sm[0:n, 0:1], acc[0:n, 128:129]), [acc_b], [sm_b])
                        yy, yy_b = ytmp_r.next()
                        y2, y2_b = ytmp_r.next()
                        self.ts("dve", yy[0:n, :], acc[0:n, 0:128], sm[0:n, 0:1], None, ALU.mult, None,
                                [acc_b, sm_b], [yy_b])
                        yt, yt_b = ytok_r.next()

                        def tail():
                            self.stt(yy[0:n, :], yy[0:n, :], self.neglam[l][0:n, :], y1[0:n, ci, :], ALU.mult, ALU.add,
                                     [yy_b, y1_b, self.lam_b], [yy_b])
                            self.stt(y2[0:n, :], yy[0:n, :], 1.0, yy[0:n, :], ALU.mult, ALU.mult, [yy_b], [y2_b, sm_b],
                                     accum=sm[0:n, 1:2])
                            self.rstd(sm[0:n, 3:4], sm[0:n, 1:2], 1.0 / 128, SUBLN_EPS, sm_b, sm_b, sm[0:n, 2:3], sm_b)
                            self.stt(yt[0:n, :], yy[0:n, :], sm[0:n, 3:4], gsub[0:n, :], ALU.mult, ALU.mult,
                                     [yy_b, sm_b, gsub_b], [yt_b])
                        return [tail, emit_T(yt, yt_b, n, t0, ys, ys_b)]
                    st1 = self.attn_steps(qT, q_b, kT, k_b, 0, 64, va, v_b, pt_r, None, None, None, None, None, d1,
                                          sring, accring)
                    st2_ = self.attn_steps(qT, q_b, kT, k_b, 64, 64, va, v_b, pt_r, None, None, None, None, None, d2,
                                           sring, accring)
                    steps = []
                    for a_, b_ in zip(st1, st2_):
                        steps.append(a_)
                        steps.append(b_)
                    steps.append(lambda: self.dma("sp", self.yT_scr[s, :, 6 + hh, :], ys[:], self.chn(), [ys_b],
                                                  [self.b_yT[s]]))
                    return steps

                heads = [("fox", hh) for hh in range(6)] + [("diff", hh) for hh in range(4)]

                def pj(kind, hh):
                    if kind == "fox":
                        return proj_steps(O_FQ + hh * 128, O_FK + hh * 128, O_FV + hh * 128, FOX_SCALE)
                    return proj_steps(O_DQ + hh * 128, O_DK + hh * 128, O_DV + hh * 128, DIFF_SCALE)

                psteps, hd = pj(*heads[0])
                for st_ in psteps:
                    st_()
                for i, (kind, hh) in enumerate(heads):
                    asteps = fox_attn(hh, hd) if kind == "fox" else diff_attn(hh, hd)
                    if i + 1 < len(heads):
                        psteps, hd_next = pj(*heads[i + 1])
                    else:
                        psteps, hd_next = [], None
                    self.run_merged(asteps, psteps)
                    hd = hd_next
                P.barrier()

            with contextlib.ExitStack() as st2:
                wa_r = self.ring(st2, "wa", [128, KC, 128], BF16, 2)
                wg_r = self.ring(st2, "wgt", [128, KC, 128], BF16, 2)
                U, U_b = self.sb(st2, "U", [128, 6, 30 + L], BF16)
                sig_r = self.ring(st2, "sig", [128, 512], F32, 3)
                self.memset("dve", U[:], 0.0, [U_b])
                for c in range(6):
                    wa, wa_b = wa_r.next()
                    wg, wg_b = wg_r.next()
                    self.wload(wa[:], w_in[:, :, O_CG + c * 128:O_CG + (c + 1) * 128], self.chwn(), wa_b)
                    self.wload(wg[:], w_in[:, :, O_CG + 768 + c * 128:O_CG + 768 + (c + 1) * 128], self.chwn(), wg_b)
                    for (t0, n) in self.wtiles:
                        pa, pa_b = self.psring.next()
                        pg, pg_b = self.psring.next()
                        for kc in range(KC):
                            self.mm(pg[:, 0:n], wg[:, kc, :], hnT[:, kc, t0:t0 + n], kc == 0, kc == KC - 1,
                                    [wg_b, hn_b], [pg_b], inc=(kc == KC - 1))
                        sg, sg_b = sig_r.next()
                        self.act(sg[:, 0:n], pg[:, 0:n], AF.Sigmoid, [pg_b], [sg_b])
                        for kc in range(KC):
                            self.mm(pa[:, 0:n], wa[:, kc, :], hnT[:, kc, t0:t0 + n], kc == 0, kc == KC - 1,
                                    [wa_b, hn_b], [pa_b], inc=(kc == KC - 1))
                        self.tt("dve", U[:, c, 30 + t0:30 + t0 + n], pa[:, 0:n], sg[:, 0:n], ALU.mult,
                                [pa_b, sg_b], [U_b])
                self.dma("sp", self.U_scr[s], U[:], self.chn(), [U_b], [self.b_U[s]])
                P.barrier()
            sth.close()
            stwo = contextlib.ExitStack()
            if self.fuse_wout:
                wo, wo_bs = self.wout_weights(stwo, l)

            with contextlib.ExitStack() as st2:
                U, U_b = self.sb(st2, "U2", [128, 6, 30 + L], BF16)
                self.dma("sp", U[:], self.U_scr[s], self.chn(), [self.b_U[s]], [U_b])
                UC_r = self.ring(st2, "UC", [128, 6, 512], F32, 2)
                sq_r = self.ring(st2, "sq", [128, 512], F32, 3)
                Dg, _ = self.sb(st2, "Dg", [128, 6, CONVW, 128], BF16)
                Dg_b = [[Buf("dg") for _ in range(CONVW)] for _ in range(6)]
                wdw, wdw_b = self.sb(st2, "wdw", [CONVW, 768], F32)
                wdT, wdT_b = self.sb(st2, "wdT", [128, 6, 32], F32)
                prm, prm_b = self.sb(st2, "cprm", [128, 3, 6], F32)
                prow, prow_b = self.sb(st2, "cprow", [18, 128], F32)
                stat_r = self.ring(st2, "cstat", [128, 512], F32, 4)
                self.dma("sp", wdw[:], self.w_dw[l], self.chn(), (), [wdw_b])
                ps, ps_b = self.psring.next()
                for c in range(6):
                    self.tr(ps[:, c * 32:c * 32 + CONVW], wdw[0:CONVW, c * 128:(c + 1) * 128],
                            self.identf[0:CONVW, 0:CONVW], [wdw_b, self.identf_b], [ps_b])
                for c in range(6):
                    self.cp("dve", wdT[:, c, 0:CONVW], ps[:, c * 32:c * 32 + CONVW], [ps_b], [wdT_b])
                for i, src in enumerate((self.b_dw, self.ln_g, self.ln_b)):
                    self.dma("sp", prow[i * 6:(i + 1) * 6, :], src[l].rearrange("(c p) -> c p", p=128), self.chn(), (),
                             [prow_b])
                ps, ps_b = self.psring.next()
                self.tr(ps[:, 0:18], prow[0:18, :], self.identf[0:18, 0:18], [prow_b, self.identf_b], [ps_b])
                self.cp("dve", prm[:].rearrange("p a c -> p (a c)"), ps[:, 0:18], [ps_b], [prm_b])
                for c in range(6):
                    for k in range(CONVW):
                        if k % 2:
                            self.P.op("act", lambda h, k=k, c=c: h.activation(
                                Dg[:, c, k, :], self.ident[:], AF.Copy, scale=wdT[:, c, k:k + 1]),
                                [self.ident_b, wdT_b], [Dg_b[c][k]])
                        else:
                            self.ts("dve", Dg[:, c, k, :], self.ident[:], wdT[:, c, k:k + 1], None,
                                    ALU.mult, None, [self.ident_b, wdT_b], [Dg_b[c][k]])

                def ln_tile(t0, n, UC, UC_b):
                    p1, p1_b = self.psring.next()
                    p2, p2_b = self.psring.next()
                    for c in range(6):
                        sq, sq_b = sq_r.next()
                        self.act(sq[:, 0:n], UC[:, c, 0:n], AF.Square, [UC_b], [sq_b])
                        self.mm(p1[:, 0:n], self.onesf[:], UC[:, c, 0:n], c == 0, c == 5, [self.onesf_b, UC_b], [p1_b],
                                inc=(c == 5))
                        self.mm(p2[:, 0:n], self.onesf[:], sq[:, 0:n], c == 0, c == 5, [self.onesf_b, sq_b], [p2_b],
                                inc=True)
                    mean, mean_b = stat_r.next()
                    var, var_b = stat_r.next()
                    self.ts("dve", mean[:, 0:n], p1[:, 0:n], 1.0 / 768, None, ALU.mult, None, [p1_b], [mean_b])
                    self.tt("dve", var[:, 0:n], mean[:, 0:n], mean[:, 0:n], ALU.mult, [mean_b], [var_b])
                    self.stt(var[:, 0:n], p2[:, 0:n], 1.0 / 768, var[:, 0:n], ALU.mult, ALU.subtract,
                             [p2_b, var_b], [var_b])
                    self.act(var[:, 0:n], var[:, 0:n], AF.Ln, [var_b], [var_b], bias=self.epsc[EPS][:, :], scale=1.0)
                    self.act(var[:, 0:n], var[:, 0:n], AF.Exp, [var_b], [var_b], scale=-0.5)
                    for c in range(6):
                        ys, ys_b = ystage.next()
                        self.tt("dve", UC[:, c, 0:n], UC[:, c, 0:n], mean[:, 0:n], ALU.subtract, [mean_b, UC_b], [UC_b])
                        self.tt("dve", UC[:, c, 0:n], UC[:, c, 0:n], var[:, 0:n], ALU.mult, [var_b, UC_b], [UC_b])
                        self.ts("dve", UC[:, c, 0:n], UC[:, c, 0:n], prm[:, 1, c:c + 1], prm[:, 2, c:c + 1], ALU.mult,
                                ALU.add, [prm_b, UC_b], [UC_b])
                        self.act(ys[:, 0:n], UC[:, c, 0:n], AF.Silu, [UC_b], [ys_b])
                        self.dma("sp", self.yT_scr[s, :, 10 + c, t0:t0 + n], ys[:, 0:n], self.chn(), [ys_b],
                                 [self.b_yT[s]])

                prev = None
                for (t0, n) in self.wtiles:
                    UC, UC_b = UC_r.next()
                    for c in range(6):
                        pc, pc_b = self.psring.next()
                        for k in range(CONVW):
                            self.mm(pc[:, 0:n], Dg[:, c, k, :], U[:, c, t0 + k:t0 + k + n], k == 0, k == CONVW - 1,
                                    [Dg_b[c][k], U_b], [pc_b], inc=(k == CONVW - 1))
                        self.ts("dve", UC[:, c, 0:n], pc[:, 0:n], prm[:, 0, c:c + 1], None, ALU.add, None,
                                [pc_b, prm_b], [UC_b])
                    if prev is not None:
                        ln_tile(*prev)
                    prev = (t0, n, UC, UC_b)
                ln_tile(*prev)
                P.barrier()
            if self.fuse_wout:
                self.wout_body(s, l, wo, wo_bs)
            stwo.close()
        P.barrier()

    def wout_weights(self, st, l):
        wo, _ = self.sb(st, "wo", [128, KC, D], BF16)
        wsrc = self.w_out[l].rearrange("(kc p) n -> p kc n", p=128)
        wo_bs = [Buf("wo%d" % i) for i in range(4)]
        for pn in range(4):
            self.wload(wo[:, :, pn * 512:(pn + 1) * 512], wsrc[:, :, pn * 512:(pn + 1) * 512], self.chwn(), wo_bs[pn])
        return wo, wo_bs

    def phase_wout(self, s, l):
        if self.fuse_wout:
            return
        with contextlib.ExitStack() as st:
            wo, wo_bs = self.wout_weights(st, l)
            self.wout_body(s, l, wo, wo_bs)

    def wout_body(self, s, l, wo, wo_bs):
        P = self.P
        L = self.L
        with contextlib.ExitStack() as st:
            yin_r = self.ring(st, "yin", [128, KC, 128], BF16, 3)
            hring = self.ring(st, "wh", [128, D], F32, 3)
            tring = self.ring(st, "wt", [128, D], F32, 2)
            yring = self.ring(st, "wy", [128, D], BF16, 4)
            stg = self.ring(st, "wstg", [128, KC, 128], BF16, 2)
            junk, junk_b = self.sb(st, "wjunk", [128, D], BF16)
            smr = self.ring(st, "wsm", [128, 16], F32, 4)
            gpost, gpost_b = self.gload(st, "g_post_mix", l)
            gpre, gpre_b = self.gload(st, "g_pre_ffn", l)
            def pre_w(ci):
                t0, n = self.chunks[ci]
                yin, yin_b = yin_r.next()
                nl_ = max(n, 128)
                self.dma("sp", yin[:, :, 0:nl_], self.yT_scr[s, :, :, t0:t0 + nl_], self.chn(), [self.b_yT[s]], [yin_b])
                h, h_b = hring.next()
                self.dma("sp", h[0:n, :], self.hres[s, t0:t0 + n, :], self.chn(), [self.b_hres[s]], [h_b])
                return yin, yin_b, h, h_b
            nxt = pre_w(0)
            pend = []
            for ci, (t0, n) in enumerate(self.chunks):
                yin, yin_b, h, h_b = nxt
                if ci + 1 < self.NCH:
                    nxt = pre_w(ci + 1)
                sm, sm_b = smr.next()
                t, t_b = tring.next()
                banks = []
                for pn in range(4):
                    if pn == 2 and len(pend) >= 2:
                        pend.pop(0)()
                    ps, ps_b = self.psring.next()
                    banks.append((ps, ps_b))
                    for kc in range(KC):
                        self.mm(ps[0:n, :], yin[:, kc, 0:n], wo[:, kc, pn * 512:(pn + 1) * 512], kc == 0, kc == KC - 1,
                                [yin_b, wo_bs[pn]], [ps_b], inc=(kc == KC - 1))
                    self.cp("act" if pn % 2 else "dve", t[0:n, pn * 512:(pn + 1) * 512], ps[0:n, 0:512], [ps_b], [t_b])
                self.act(junk[0:n, :], t[0:n, :], AF.Square, [t_b], [junk_b, sm_b], accum=sm[0:n, 0:1])
                self.rstd(sm[0:n, 2:3], sm[0:n, 0:1], 1.0 / D, EPS, sm_b, sm_b, sm[0:n, 1:2], sm_b)
                self.stt(t[0:n, :], t[0:n, :], sm[0:n, 2:3], gpost[0:n, :], ALU.mult, ALU.mult, [sm_b, gpost_b, t_b], [t_b])
                self.tt("dve", h[0:n, :], h[0:n, :], t[0:n, :], ALU.add, [t_b, h_b], [h_b])
                self.dma("sp", self.hres[s, t0:t0 + n, :], h[0:n, :], self.chn(), [h_b], [self.b_hres[s]])
                pend.append(self.prenorm_T(h, h_b, n, t0, gpre, gpre_b, junk, junk_b, sm, sm_b, yring, stg,
                                           self.hnT_scr[s], self.b_hnT[s], defer=True))
            while pend:
                pend.pop(0)()
        P.barrier()

    def phase_ffn1(self, s, l):
        P = self.P
        L = self.L
        wg_src = self.w_gate[l].rearrange("(kc p) n -> p kc n", p=128)
        wu_src = self.w_up[l].rearrange("(kc p) n -> p kc n", p=128)
        with contextlib.ExitStack() as st:
            hnT, _ = self.sb(st, "hnTf", [128, KC, L], BF16)
            hn_bs = {}
            for (t0, n) in self.wtiles:
                hn_bs[t0] = Buf("hnTf%d" % t0)
                self.dma("sp", hnT[:, :, t0:t0 + n], self.hnT_scr[s, :, :, t0:t0 + n], self.chn(), [self.b_hnT[s]],
                         [hn_bs[t0]])
            wg_r = self.ring(st, "fwg", [128, KC, 512], BF16, 2)
            wu_r = self.ring(st, "fwu", [128, KC, 512], BF16, 2)
            wcT_r = self.ring(st, "fwcT", [128, 4, 4], F32, 2)
            wcr_r = self.ring(st, "fwcr", [3, 512], F32, 2)
            G_r = self.ring(st, "fG", [128, 2 + L], BF16, 2)
            for (t_, b_) in G_r.items:
                self.memset("dve", t_[:], 0.0, [b_])
            cv_r = self.ring(st, "fcv", [128, 512], F32, 3)
            sl_r = self.ring(st, "fsl", [128, 512], F32, 3)
            a_r = self.ring(st, "faT4", [128, self.NCH, 4, 128], BF16, 2)
            for (t_, b_) in a_r.items:
                self.memset("dve", t_[:, 0, :, :], 0.0, [b_])
            aT_dst = self.aT_scr[s].rearrange("c p f t -> p c f t")
            fpend = [None]
            wc_state = {}

            def wc_load(g):
                wcr, wcr_b = wcr_r.next()
                self.dma("sp", wcr[:], self.w_fc[l][:, g * 512:(g + 1) * 512], self.chn(), (), [wcr_b])
                wc_state["r"] = (wcr, wcr_b)

            def wc_T():
                wcr, wcr_b = wc_state["r"]
                wcT, wcT_b = wcT_r.next()
                pw, pw_b = self.ptfring.next()
                for q in range(4):
                    self.tr(pw[:, q * 4:q * 4 + 3], wcr[0:3, q * 128:(q + 1) * 128], self.identf[0:3, 0:3],
                            [wcr_b, self.identf_b], [pw_b])
                for q in range(4):
                    self.cp("dve", wcT[:, q, 0:3], pw[:, q * 4:q * 4 + 3], [pw_b], [wcT_b])
                wc_state["T"] = (wcT, wcT_b)
            for g in range(FCH // 4):
                wg, wg_b = wg_r.next()
                wu, wu_b = wu_r.next()
                aT, aT_b = a_r.next()
                if g == 0:
                    wg_bl = [Buf("wg0%d" % q) for q in range(4)]
                    wu_bl = [Buf("wu0%d" % q) for q in range(4)]
                    for q in range(4):
                        self.wload(wg[:, :, q * 128:(q + 1) * 128], wg_src[:, :, q * 128:(q + 1) * 128], self.chwn(),
                                   wg_bl[q])
                        self.wload(wu[:, :, q * 128:(q + 1) * 128], wu_src[:, :, q * 128:(q + 1) * 128], self.chwn(),
                                   wu_bl[q])
                    wg_bl = [[b_, wg_b] for b_ in wg_bl]
                    wu_bl = [[b_, wu_b] for b_ in wu_bl]
                else:
                    self.wload(wg[:], wg_src[:, :, g * 512:(g + 1) * 512], self.chwn(), wg_b)
                    self.wload(wu[:], wu_src[:, :, g * 512:(g + 1) * 512], self.chwn(), wu_b)
                    wg_bl = [[wg_b]] * 4
                    wu_bl = [[wu_b]] * 4
                if g == 0:
                    wc_load(0)
                    wc_T()
                wcT, wcT_b = wc_state["T"]
                if g + 1 < FCH // 4:
                    wc_load(g + 1)
                for q in range(4):
                    fc = g * 4 + q
                    G, G_b = G_r.next()
                    for (t0, n) in self.wtiles:
                        pg, pg_b = self.psring.next()
                        pu, pu_b = self.psring.next()
                        for kc in range(KC):
                            self.mm(pg[:, 0:n], wg[:, kc, q * 128:(q + 1) * 128], hnT[:, kc, t0:t0 + n], kc == 0,
                                    kc == KC - 1, wg_bl[q] + [hn_bs[t0]], [pg_b], inc=(kc == KC - 1))
                        self.cp("act", G[:, 2 + t0:2 + t0 + n], pg[:, 0:n], [pg_b], [G_b])
                        for kc in range(KC):
                            self.mm(pu[:, 0:n], wu[:, kc, q * 128:(q + 1) * 128], hnT[:, kc, t0:t0 + n], kc == 0,
                                    kc == KC - 1, wu_bl[q] + [hn_bs[t0]], [pu_b], inc=(kc == KC - 1))
                        cv, cv_b = cv_r.next()
                        self.P.op("act", lambda h, cv=cv, pg=pg, n=n, wcT=wcT, q=q: h.activation(
                            cv[:, 0:n], pg[:, 0:n], AF.Copy, scale=wcT[:, q, 2:3]), [pg_b, wcT_b], [cv_b])
                        self.stt(cv[:, 0:n], G[:, t0 + 1:t0 + 1 + n], wcT[:, q, 1:2], cv[:, 0:n], ALU.mult, ALU.add,
                                 [G_b, wcT_b, cv_b], [cv_b])
                        self.stt(cv[:, 0:n], G[:, t0:t0 + n], wcT[:, q, 0:1], cv[:, 0:n], ALU.mult, ALU.add,
                                 [G_b, wcT_b, cv_b], [cv_b])
                        sl, sl_b = sl_r.next()
                        self.act(sl[:, 0:n], cv[:, 0:n], AF.Silu, [cv_b], [sl_b])
                        if fpend[0] is not None:
                            fpend[0]()

                        def fmul(aT=aT, aT_b=aT_b, q=q, t0=t0, n=n, pu=pu, pu_b=pu_b, sl=sl, sl_b=sl_b):
                            if n == 16:
                                self.tt("dve", aT[:, 0, q, 0:16], pu[:, 0:16], sl[:, 0:16], ALU.mult, [pu_b, sl_b], [aT_b])
                            else:
                                c0 = 1 + (t0 - 16) // 128
                                self.tt("dve", aT[:, c0:c0 + 4, q, :], pu[:, 0:512].rearrange("p (c t) -> p c t", t=128),
                                        sl[:, 0:512].rearrange("p (c t) -> p c t", t=128), ALU.mult, [pu_b, sl_b], [aT_b])
                        fpend[0] = fmul
                if fpend[0] is not None:
                    fpend[0]()
                    fpend[0] = None
                self.dma("sp", aT_dst[:, :, g * 4:(g + 1) * 4, :], aT[:], self.chn(), [aT_b], [self.b_aT[s]])
                if g + 1 < FCH // 4:
                    wc_T()
        P.barrier()

    def down_steps(self, st, s, l, nain=3):
        wd_src = self.w_down[l].rearrange("(fc p) n -> p fc n", p=128)
        wd_r = self.ring(st, "wd", [128, FCH, 512], BF16, 2)
        ain_r = self.ring(st, "ain", [128, FCH, 128], BF16, nain)
        o_r = self.ring(st, "dout", [128, 512], F32, 3)
        state = {"nxt": None, "wd": None}

        def pre_d(ci):
            t0, n = self.chunks[ci]
            ain, ain_b = ain_r.next()
            self.dma("sp", ain[:], self.aT_scr[s, ci], self.chn(), [self.b_aT[s]], [ain_b])
            return ain, ain_b

        def step(pn, ci):
            t0, n = self.chunks[ci]
            if ci == 0:
                wd, wd_b = wd_r.next()
                for q in range(4):
                    self.wload(wd[:, q * 11:(q + 1) * 11, :], wd_src[:, q * 11:(q + 1) * 11, pn * 512:(pn + 1) * 512],
                               self.chwn(), wd_b)
                state["wd"] = (wd, wd_b)
                if pn == 0:
                    state["nxt"] = pre_d(0)
            wd, wd_b = state["wd"]
            ain, ain_b = state["nxt"]
            if ci + 1 < self.NCH:
                state["nxt"] = pre_d(ci + 1)
            elif pn < 3:
                state["nxt"] = pre_d(0)
            ps, ps_b = self.psring.next()
            for fc in range(FCH):
                self.mm(ps[0:n, :], ain[:, fc, 0:n], wd[:, fc, :], fc == 0, fc == FCH - 1, [ain_b, wd_b], [ps_b],
                        inc=(fc == FCH - 1))
            o, o_b = o_r.next()
            self.cp("act" if ci % 2 else "dve", o[0:n, :], ps[0:n, :], [ps_b], [o_b])
            self.dma("sp", self.fraw[s, t0:t0 + n, pn * 512:(pn + 1) * 512], o[0:n, :], self.chn(), [o_b],
                     [self.b_fraw[s]])
        return [(lambda pn=pn, ci=ci: step(pn, ci)) for pn in range(4) for ci in range(self.NCH)]

    def phase_down(self, s, l, norm=None):
        with contextlib.ExitStack() as st:
            if norm is None:
                for st_ in self.down_steps(st, s, l):
                    st_()
            else:
                dsteps = self.down_steps(st, s, l, nain=2)
                nsteps = self.norm_steps(st, norm[0], norm[1], norm[2], nbuf=2)
                self.run_merged(dsteps, nsteps)
        self.P.barrier()


_CACHE = {}


def _consts():
    ident = np.eye(128, dtype=np.float32)
    kk = np.arange(128)[:, None]
    qq = np.arange(128)[None, :]
    mask = np.where(kk <= qq, 0.0, NEG).astype(np.float32)
    sel3 = np.zeros((70, 6, 128), np.float32)
    for h in range(6):
        for r in (0, 32, 64):
            sel3[r + h, h, :] = 1.0
    return ident, mask, sel3.reshape(70, 768)


def run(inputs, NS, SEQ, n_cores, stop_after=None, debug=False, trace=False):
    b = Builder(NS, SEQ, stop_after=stop_after, debug=debug)
    nc = b.build()
    ident, mask, sel3 = _consts()
    x = np.ascontiguousarray(inputs["x"], dtype=np.float32)
    in_maps = []
    for c in range(n_cores):
        m = {k: np.ascontiguousarray(v, dtype=np.float32) for k, v in inputs.items() if k != "x"}
        m["x"] = np.ascontiguousarray(x[c * NS:(c + 1) * NS, :SEQ])
        m["c_ident"] = ident
        m["c_mask"] = mask
        m["c_sel3"] = sel3
        in_maps.append(m)
    res = run_bass_kernel_spmd(nc, in_maps, core_ids=list(range(n_cores)), trace=trace)
    return res, b


def kernel(**inputs):
    res, _ = run(inputs, 2, 2048, 8)
    return np.concatenate([r["out"] for r in res.results], axis=0).astype(np.float32)
```
